# Optimizing a Trainium2 kernel written in Bass

```python
import math
import jax, jax.numpy as jnp
from jax import lax
import numpy as np

D_MODEL = 1024
BATCH = 4
SEQ = 8192
DEPTH = 4

HEAD_DIM = 64
HG_W = D_MODEL // 4
HG_HEADS = HG_W // HEAD_DIM
NSA_W = D_MODEL // 2
NSA_HEADS = NSA_W // HEAD_DIM
NSA_GQA = 4
NSA_KV_HEADS = NSA_HEADS // NSA_GQA
KV_W = NSA_KV_HEADS * HEAD_DIM
GM_W = D_MODEL - HG_W - NSA_W
GM_HEADS = GM_W // HEAD_DIM
D_MIX = HG_W + NSA_W + GM_W
IN_COLS = 4 * HG_W + NSA_W + 6 * KV_W + 3 * NSA_HEADS + 2 * GM_W
D_FF = 4 * D_MODEL
ROPE_THETA = 10000.0
HGRN_CHUNK = 64
CMP_LEN = 32
CMP_STRIDE = 16
CMP_HIDDEN = 256
SLC_BLOCK = 64
SLC_TOPN = 16
WINDOW = 512
Q_BLOCK = 128
GMLP_CHUNK = 128
DN_ALPHA = (2 * DEPTH) ** 0.25
DN_BETA = (8 * DEPTH) ** -0.25
NEG = -1e30
BIG = 1e30
F_MIN = 1e-30

kernel_name = 'hymba_style_hgrn2_nsa_gmlp_deepnorm_adaln'


def layer_norm(x, w, b, eps=1e-5):
    xf = x.astype(jnp.float32)
    mu = jnp.mean(xf, axis=-1, keepdims=True)
    var = jnp.mean(jnp.square(xf - mu), axis=-1, keepdims=True)
    y = (xf - mu) * lax.rsqrt(var + eps)
    return (y * w.astype(jnp.float32) + b.astype(jnp.float32)).astype(x.dtype)


def rope(a, cos, sin):
    half = a.shape[-1] // 2
    a1, a2 = a[..., :half], a[..., half:]
    return jnp.concatenate([a1 * cos - a2 * sin, a2 * cos + a1 * sin], axis=-1).astype(a.dtype)


def hgrn2(q, f_raw, i, g, lb, norm_w):
    B_, S_, H, dk = q.shape
    dt = q.dtype
    C = HGRN_CHUNK
    lb = lb.astype(jnp.float32).reshape(H, dk)
    z = f_raw.astype(jnp.float32)
    f = lb + (1.0 - lb) * jax.nn.sigmoid(z)
    log_f = jnp.log(jnp.maximum(f, F_MIN))
    k = (1.0 - lb) * jax.nn.sigmoid(-z)
    qf = jax.nn.silu(q.astype(jnp.float32))
    v = i.astype(jnp.float32)
    nC = S_ // C

    def chunks(a):
        return a.reshape(B_, nC, C, H, dk).transpose(1, 0, 3, 2, 4)

    qc, kc, vc = chunks(qf), chunks(k), chunks(v)
    bc = jnp.cumsum(chunks(log_f), axis=3)
    causal = jnp.tril(jnp.ones((C, C), bool))

    def step(state, inp):
        q_, k_, v_, b_ = inp
        rel = b_[:, :, :, None, :] - b_[:, :, None, :, :]
        decay = jnp.exp(jnp.where(causal[:, :, None], rel, NEG))
        attn = jnp.einsum('bhtk,bhtsk,bhsk->bhts', q_, decay, k_)
        o = attn @ v_ + jnp.einsum('bhtk,bhkv->bhtv', q_ * jnp.exp(b_), state)
        b_last = b_[:, :, -1:, :]
        state = (jnp.exp(b_last[:, :, 0, :])[..., None] * state
                 + jnp.einsum('bhsk,bhsv->bhkv', k_ * jnp.exp(b_last - b_), v_))
        return state, o

    s0 = jnp.zeros((B_, H, dk, dk), jnp.float32)
    _, o = lax.scan(step, s0, (qc, kc, vc, bc))
    o = o.transpose(1, 0, 3, 2, 4).reshape(B_, S_, H, dk)
    o = o * lax.rsqrt(jnp.mean(jnp.square(o), axis=-1, keepdims=True) + 1e-6) * norm_w.astype(jnp.float32)
    o = o * jax.nn.silu(g.astype(jnp.float32))
    return o.reshape(B_, S_, H * dk).astype(dt)


def compress(a, pe, w1, w2):
    B_, S_, Hk, dh = a.shape
    seg = a.reshape(B_, S_ // CMP_STRIDE, CMP_STRIDE, Hk, dh)
    blk = jnp.concatenate([seg[:, :-1], seg[:, 1:]], axis=2)
    blk = blk + pe[None, None, :, None, :]
    nC = blk.shape[1]
    blk = blk.transpose(0, 1, 3, 2, 4).reshape(B_, nC, Hk, CMP_LEN * dh)
    return jax.nn.gelu(blk @ w1) @ w2


def nsa(q, k_cmp, v_cmp, k_slc, v_slc, k_win, v_win, gates, pe_k, w1k, w2k, pe_v, w1v, w2v):
    B_, S_, H, dh = q.shape
    Hk, G = NSA_KV_HEADS, NSA_GQA
    dt = q.dtype
    qg = (q * (dh ** -0.5)).reshape(B_, S_, Hk, G, dh)
    kc = compress(k_cmp, pe_k, w1k, w2k)
    vc = compress(v_cmp, pe_v, w1v, w2v)
    nC = kc.shape[1]
    nS = S_ // SLC_BLOCK
    topn = min(SLC_TOPN, nS)
    cmp_end = jnp.arange(nC) * CMP_STRIDE + CMP_LEN - 1
    cs = np.arange(nC) * CMP_STRIDE
    ss = np.arange(nS) * SLC_BLOCK
    overlap = jnp.asarray(((cs[:, None] < ss[None, :] + SLC_BLOCK) & (ss[None, :] < cs[:, None] + CMP_LEN)).astype(np.float32))
    ks_b = k_slc.reshape(B_, nS, SLC_BLOCK, Hk, dh).transpose(0, 3, 1, 2, 4)
    vs_b = v_slc.reshape(B_, nS, SLC_BLOCK, Hk, dh).transpose(0, 3, 1, 2, 4)
    kw_pad = jnp.pad(k_win, ((0, 0), (WINDOW, 0), (0, 0), (0, 0)))
    vw_pad = jnp.pad(v_win, ((0, 0), (WINDOW, 0), (0, 0), (0, 0)))
    bidx = jnp.arange(B_)[:, None, None, None]
    hidx = jnp.arange(Hk)[None, :, None, None]
    j = jnp.arange(nS)
    in_blk = jnp.arange(SLC_BLOCK)
    win_off = jnp.arange(WINDOW + Q_BLOCK) - WINDOW

    def block(qb):
        q0 = qb * Q_BLOCK
        t = q0 + jnp.arange(Q_BLOCK)
        q_blk = lax.dynamic_slice_in_dim(qg, q0, Q_BLOCK, axis=1)
        g_blk = lax.dynamic_slice_in_dim(gates, q0, Q_BLOCK, axis=1)
        s = jnp.einsum('bqhgd,bnhd->bhgqn', q_blk, kc).astype(jnp.float32)
        m = cmp_end[None, :] <= t[:, None]
        p_cmp = jax.nn.softmax(jnp.where(m, s, NEG), axis=-1) * m
        o_cmp = jnp.einsum('bhgqn,bnhd->bqhgd', p_cmp.astype(dt), vc)
        imp = jnp.einsum('bhgqn,nj->bhqj', p_cmp, overlap)
        tb = t // SLC_BLOCK
        forced = (j[None] == 0) | (j[None] == tb[:, None]) | (j[None] == tb[:, None] - 1)
        score = jnp.where(j[None] > tb[:, None], NEG, jnp.where(forced, BIG, imp))
        _, idx = lax.top_k(score, topn)
        k_sel = ks_b[bidx, hidx, idx].reshape(B_, Hk, Q_BLOCK, topn * SLC_BLOCK, dh)
        v_sel = vs_b[bidx, hidx, idx].reshape(B_, Hk, Q_BLOCK, topn * SLC_BLOCK, dh)
        kpos = (idx[..., None] * SLC_BLOCK + in_blk).reshape(B_, Hk, Q_BLOCK, topn * SLC_BLOCK)
        s = jnp.einsum('bqhgd,bhqkd->bhgqk', q_blk, k_sel).astype(jnp.float32)
        m = (kpos <= t[None, None, :, None])[:, :, None]
        p = jax.nn.softmax(jnp.where(m, s, NEG), axis=-1)
        o_slc = jnp.einsum('bhgqk,bhqkd->bqhgd', p.astype(dt), v_sel)
        k_w = lax.dynamic_slice_in_dim(kw_pad, q0, WINDOW + Q_BLOCK, axis=1)
        v_w = lax.dynamic_slice_in_dim(vw_pad, q0, WINDOW + Q_BLOCK, axis=1)
        kp = q0 + win_off
        m = (kp[None] <= t[:, None]) & (kp[None] > t[:, None] - WINDOW) & (kp[None] >= 0)
        s = jnp.einsum('bqhgd,bkhd->bhgqk', q_blk, k_w).astype(jnp.float32)
        p = jax.nn.softmax(jnp.where(m, s, NEG), axis=-1)
        o_win = jnp.einsum('bhgqk,bkhd->bqhgd', p.astype(dt), v_w)
        out = g_blk[..., 0:1] * o_cmp + g_blk[..., 1:2] * o_slc + g_blk[..., 2:3] * o_win
        return out.astype(dt)

    out = lax.map(block, jnp.arange(S_ // Q_BLOCK))
    return out.transpose(1, 0, 2, 3, 4, 5).reshape(B_, S_, H * dh)


def gmlp(u_raw, v_raw, nw, nb, w_s, b_s):
    B_, S_, W = u_raw.shape
    u = jax.nn.gelu(u_raw)
    v = layer_norm(jax.nn.gelu(v_raw), nw, nb)
    v = v.reshape(B_, S_ // GMLP_CHUNK, GMLP_CHUNK, GM_HEADS, W // GM_HEADS)
    w = w_s * jnp.tril(jnp.ones((GMLP_CHUNK, GMLP_CHUNK), w_s.dtype))
    sv = jnp.einsum('gts,bnsgc->bntgc', w, v) + b_s.T[None, None, :, :, None]
    return u * sv.reshape(B_, S_, W)


def setup_inputs(seed: int = 0) -> dict:
    key = jax.random.key(seed)
    ks = jax.random.split(key, 32)
    L, D = DEPTH, D_MODEL

    def nrm(k, shape, scale):
        return jax.random.normal(k, shape, jnp.float32) * scale

    offs = jax.random.randint(ks[2], (BATCH, 1), 0, 4096, dtype=jnp.int32)
    return {
        'x': nrm(ks[0], (BATCH, SEQ, D), 1.0),
        'c': nrm(ks[1], (BATCH, D), 1.0),
        'positions': offs + jnp.arange(SEQ, dtype=jnp.int32)[None, :],
        'w_in': nrm(ks[3], (L, D, IN_COLS), D ** -0.5),
        'w_o': nrm(ks[4], (L, D_MIX, D), D_MIX ** -0.5 * DN_BETA),
        'hgrn_lower_bounds': 1.0 + nrm(ks[5], (L, HG_W), 0.1),
        'hgrn_norm_w': 1.0 + nrm(ks[6], (L, HEAD_DIM), 0.02),
        'cmp_pe_k': nrm(ks[7], (L, CMP_LEN, HEAD_DIM), 0.02),
        'cmp_w1_k': nrm(ks[8], (L, CMP_LEN * HEAD_DIM, CMP_HIDDEN), (CMP_LEN * HEAD_DIM) ** -0.5),
        'cmp_w2_k': nrm(ks[9], (L, CMP_HIDDEN, HEAD_DIM), CMP_HIDDEN ** -0.5),
        'cmp_pe_v': nrm(ks[10], (L, CMP_LEN, HEAD_DIM), 0.02),
        'cmp_w1_v': nrm(ks[11], (L, CMP_LEN * HEAD_DIM, CMP_HIDDEN), (CMP_LEN * HEAD_DIM) ** -0.5),
        'cmp_w2_v': nrm(ks[12], (L, CMP_HIDDEN, HEAD_DIM), CMP_HIDDEN ** -0.5),
        'gmlp_norm_w': 1.0 + nrm(ks[13], (L, GM_W), 0.02),
        'gmlp_norm_b': nrm(ks[14], (L, GM_W), 0.02),
        'gmlp_w_s': nrm(ks[15], (L, GM_HEADS, GMLP_CHUNK, GMLP_CHUNK), GMLP_CHUNK ** -0.5),
        'gmlp_b_s': 1.0 + nrm(ks[16], (L, GM_HEADS, GMLP_CHUNK), 0.02),
        'w_ff1': nrm(ks[17], (L, D, D_FF), D ** -0.5),
        'w_ff2': nrm(ks[18], (L, D_FF, D), D_FF ** -0.5 * DN_BETA),
        'w_ada': nrm(ks[19], (L, D, 6 * D), 0.1 * D ** -0.5),
        'b_ada': nrm(ks[20], (L, 6 * D), 0.02),
        'ln1_w': 1.0 + nrm(ks[21], (L, D), 0.02),
        'ln1_b': nrm(ks[22], (L, D), 0.02),
        'ln2_w': 1.0 + nrm(ks[23], (L, D), 0.02),
        'ln2_b': nrm(ks[24], (L, D), 0.02),
    }


def reference(x, c, positions, w_in, w_o, hgrn_lower_bounds, hgrn_norm_w,
              cmp_pe_k, cmp_w1_k, cmp_w2_k, cmp_pe_v, cmp_w1_v, cmp_w2_v,
              gmlp_norm_w, gmlp_norm_b, gmlp_w_s, gmlp_b_s,
              w_ff1, w_ff2, w_ada, b_ada, ln1_w, ln1_b, ln2_w, ln2_b):
    B_, S_, D = x.shape
    inv_freq = ROPE_THETA ** (-jnp.arange(0, HEAD_DIM, 2, dtype=jnp.float32) / HEAD_DIM)
    ang = positions.astype(jnp.float32)[..., None] * inv_freq
    cos, sin = jnp.cos(ang)[:, :, None, :], jnp.sin(ang)[:, :, None, :]
    lb_sm = jax.nn.softmax(hgrn_lower_bounds.astype(jnp.float32), axis=0)
    lb_all = jnp.cumsum(lb_sm, axis=0) - lb_sm[0:1]
    sizes = [HG_W] * 4 + [NSA_W] + [KV_W] * 6 + [3 * NSA_HEADS] + [GM_W] * 2
    cuts = [int(v) for v in np.cumsum(sizes)[:-1]]
    ada_in = jax.nn.silu(c)
    for l in range(DEPTH):
        mod = (ada_in @ w_ada[l] + b_ada[l])[:, None, :]
        sh1, sc1, g1, sh2, sc2, g2 = jnp.split(mod, 6, axis=-1)
        h = x * (1.0 + sc1) + sh1
        proj = h @ w_in[l]
        (hq, hf, hi, hg, nq, kcm, vcm, ksl, vsl, kwn, vwn, ngt, gu, gv) = jnp.split(proj, cuts, axis=-1)
        hs = lambda a, n: a.reshape(B_, S_, n, HEAD_DIM)
        o_h = hgrn2(hs(hq, HG_HEADS), hs(hf, HG_HEADS), hs(hi, HG_HEADS), hs(hg, HG_HEADS),
                    lb_all[l], hgrn_norm_w[l])
        q_n = rope(hs(nq, NSA_HEADS), cos, sin)
        gates = jax.nn.sigmoid(ngt.astype(jnp.float32)).reshape(B_, S_, NSA_KV_HEADS, NSA_GQA, 3)
        o_n = nsa(q_n,
                  rope(hs(kcm, NSA_KV_HEADS), cos, sin), hs(vcm, NSA_KV_HEADS),
                  rope(hs(ksl, NSA_KV_HEADS), cos, sin), hs(vsl, NSA_KV_HEADS),
                  rope(hs(kwn, NSA_KV_HEADS), cos, sin), hs(vwn, NSA_KV_HEADS),
                  gates, cmp_pe_k[l], cmp_w1_k[l], cmp_w2_k[l], cmp_pe_v[l], cmp_w1_v[l], cmp_w2_v[l])
        o_g = gmlp(gu, gv, gmlp_norm_w[l], gmlp_norm_b[l], gmlp_w_s[l], gmlp_b_s[l])
        mix = jnp.concatenate([o_h, o_n, o_g], axis=-1) @ w_o[l]
        x = layer_norm(DN_ALPHA * x + (1.0 + g1) * mix, ln1_w[l], ln1_b[l])
        h = x * (1.0 + sc2) + sh2
        y = jnp.square(jax.nn.relu(h @ w_ff1[l])) @ w_ff2[l]
        x = layer_norm(DN_ALPHA * x + (1.0 + g2) * y, ln2_w[l], ln2_b[l])
    return x
```

```python
import numpy as np
from contextlib import ExitStack
import ml_dtypes
import concourse.bass as bass
import concourse.mybir as mybir
from concourse.bass_utils import run_bass_kernel_spmd

F32 = mybir.dt.float32
BF16 = mybir.dt.bfloat16
I32 = mybir.dt.int32
AF = mybir.ActivationFunctionType
ALU = mybir.AluOpType
AX = mybir.AxisListType
NPBF = ml_dtypes.bfloat16

D_MODEL = 1024
D_FF = 4096
DEPTH = 4
DN_ALPHA = (2 * DEPTH) ** 0.25
ENG = ['pe', 'act', 'dve', 'pool', 'sp']


class NS:
    def __init__(self, d):
        self.__dict__.update(d)


class Tok:
    __slots__ = ('w', 'r', 'name', 'ps')

    def __init__(self, name='', ps=False):
        self.w = None
        self.r = {}
        self.name = name
        self.ps = ps


class Prog:
    def __init__(self, nc):
        self.nc = nc
        self.es = ExitStack()
        self.ses = ExitStack()
        self.pfx = ''
        self.q = {e: [] for e in ENG}
        self.cnt = {}
        self.seen = {e: {} for e in ENG}
        self.semh = {}
        self.isdma = set()
        self.ntok = 0
        self._rec = None

    def tok(self, name='', ps=None):
        if ps is None:
            ps = name.startswith('p')
        return Tok(name, ps)

    def toks(self, n, name=''):
        return [Tok(name + str(i)) for i in range(n)]

    def sem(self, key):
        if key not in self.semh:
            self.semh[key] = self.ses.enter_context(self.nc.semaphore(key))
            self.cnt[key] = 0
        return self.semh[key]

    def sb(self, name, shape, dt):
        return self.es.enter_context(self.nc.sbuf_tensor(self.pfx + "sb_" + name, list(shape), dt))

    def ps(self, name, shape, dt):
        return self.es.enter_context(self.nc.psum_tensor(self.pfx + "ps_" + name, list(shape), dt))

    def _deps(self, eng, reads, writes):
        waits = {}

        def need(ev):
            if ev is None:
                return
            k, v = ev
            if eng == 'pe' and k == 's_pe':
                return
            if k in self.isdma:
                v = self.cnt[k]
            if self.seen[eng].get(k, 0) >= v:
                return
            if waits.get(k, 0) < v:
                waits[k] = v

        own = 's_' + eng
        for t in reads:
            need(t.w)
            if t.ps:
                for k, v in t.r.items():
                    if k != own:
                        need((k, v))
        for t in writes:
            need(t.w)
            for k, v in t.r.items():
                need((k, v))
        for k, v in waits.items():
            self.seen[eng][k] = v
        return list(waits.items())

    def _commit(self, ev, reads, writes):
        k, v = ev
        for t in reads:
            if t.r.get(k, 0) < v:
                t.r[k] = v
        for t in writes:
            t.w = ev
            t.r = {}

    def record(self, fn):
        saved, self._rec = self._rec, []
        fn()
        out, self._rec = self._rec, saved
        return out

    def atomic(self, fn):
        if self._rec is None:
            fn()
            return
        grp = self.record(fn)
        flat = []
        for kd, ar in grp:
            if kd == 'grp':
                flat.extend(ar)
            else:
                flat.append((kd, ar))
        self._rec.append(('grp', flat))

    def replay(self, *streams, gran=1, speeds=None):
        if speeds is None:
            speeds = [1.0] * len(streams)
        speeds = [sp for st, sp in zip(streams, speeds) if st]
        streams = [st for st in streams if st]
        if gran > 1:
            streams = [[('grp', st[i:i + gran]) for i in range(0, len(st), gran)] for st in streams]
        idx = [0] * len(streams)
        while True:
            best, bf = -1, 2.0
            for i, st in enumerate(streams):
                if idx[i] < len(st):
                    f = idx[i] / len(st) / speeds[i]
                    if f < bf:
                        best, bf = i, f
            if best < 0:
                break
            kind, args = streams[best][idx[best]]
            idx[best] += 1
            for kd, ar in (args if kind == 'grp' else [(kind, args)]):
                if kd == 'op':
                    self.op(*ar)
                else:
                    self.dma(*ar)

    def op(self, eng, fn, reads=(), writes=()):
        if self._rec is not None:
            self._rec.append(('op', (eng, fn, reads, writes)))
            return
        waits = self._deps(eng, reads, writes)
        key = 's_' + eng
        self.sem(key)
        self.cnt[key] += 1
        ev = (key, self.cnt[key])
        self.q[eng].append((waits, fn, key, 1))
        self._commit(ev, reads, writes)

    def dma(self, eng, out, in_, reads=(), writes=(), key=None):
        if self._rec is not None:
            self._rec.append(('dma', (eng, out, in_, reads, writes, key)))
            return
        key = key + '_' + eng
        waits = self._deps(eng, reads, writes)
        self.sem(key)
        self.isdma.add(key)
        self.cnt[key] += 16
        ev = (key, self.cnt[key])
        self.q[eng].append((waits, (lambda e, o=out, i=in_: e.dma_start(out=o, in_=i)), key, 16))
        self._commit(ev, reads, writes)

    def coll(self, kind, op, groups, ins, outs, reads=(), writes=(), key=None):
        key = key + '_cc'
        waits = self._deps('pool', reads, writes)
        self.sem(key)
        self.isdma.add(key)
        self.cnt[key] += 16
        ev = (key, self.cnt[key])
        self.q['pool'].append((waits, (lambda e: e.collective_compute(kind, op, replica_groups=groups, ins=ins, outs=outs)),
                               key, 16))
        self._commit(ev, reads, writes)

    def mm(self, out, lhsT, rhs, start, stop, reads=(), writes=()):
        self.op('pe', lambda e: e.matmul(out, lhsT, rhs, start=start, stop=stop,
                                         skip_group_check=True), reads, writes)

    def tr(self, out, in_, ident, reads=(), writes=()):
        self.op('pe', lambda e: e.transpose(out, in_, ident), reads, writes)

    def act(self, out, in_, func, scale=1.0, bias=None, reads=(), writes=(), accum_out=None):
        def fn(e):
            kw = {}
            if bias is not None:
                kw['bias'] = bias
            if accum_out is not None:
                kw['accum_out'] = accum_out
            return e.activation(out, in_, func, scale=scale, **kw)
        self.op('act', fn, reads, writes)

    def tt(self, eng, out, a, b, op, reads=(), writes=()):
        self.op(eng, lambda e: e.tensor_tensor(out, a, b, op), reads, writes)

    def ts(self, eng, out, a, s1, s2, op0, op1=None, reads=(), writes=()):
        if op1 is None:
            self.op(eng, lambda e: e.tensor_scalar(out, a, s1, None, op0), reads, writes)
        else:
            self.op(eng, lambda e: e.tensor_scalar(out, a, s1, s2, op0, op1), reads, writes)

    def stt(self, out, a, s, b, op0, op1, reads=(), writes=()):
        self.op('dve', lambda e: e.scalar_tensor_tensor(out, a, s, b, op0, op1), reads, writes)

    def cp(self, eng, out, in_, reads=(), writes=()):
        if eng == 'act':
            self.op('act', lambda e: e.copy(out, in_), reads, writes)
        else:
            self.op(eng, lambda e: e.tensor_copy(out, in_), reads, writes)

    def barrier(self):
        snapshot = dict(self.cnt)
        for eng in ENG:
            waits = []
            for k, v in snapshot.items():
                if v == 0 or self.seen[eng].get(k, 0) >= v:
                    continue
                waits.append((k, v))
                self.seen[eng][k] = v
            key = 's_' + eng
            self.sem(key)
            self.cnt[key] += 1
            self.q[eng].append((waits, (lambda e: e.nop()), key, 1))

    def emit(self):
        nc = self.nc
        prog = self

        def mk(en):
            def body(e):
                for waits, fn, key, inc in prog.q[en]:
                    for k, v in waits:
                        e.wait_ge(prog.semh[k], v)
                    fn(e).then_inc(prog.semh[key], inc)
                if en == 'sp':
                    for k in sorted(prog.isdma):
                        e.wait_ge(prog.semh[k], prog.cnt[k])
            return body

        with nc.Block() as block:
            block.tensor(mk('pe'))
            block.scalar(mk('act'))
            block.vector(mk('dve'))
            block.gpsimd(mk('pool'))
            block.sync(mk('sp'))
        self.es.close()
        self.ses.close()

    def begin_phase(self, pfx):
        self.pfx = pfx
        self.es = ExitStack()

    def end_phase(self):
        self.barrier()
        self.es.close()
        self.es = ExitStack()


def ln_tile(P, src, dst_hat, stats, mv, sc, eps, t_src, t_dst, t_small):
    for j in range(2):
        P.op('dve', lambda e, j=j: e.bn_stats(stats[:, j, :], src[:, j * 512:(j + 1) * 512]),
             [t_src], [t_small] if j == 0 else [t_small])
    P.op('dve', lambda e: e.bn_aggr(mv[:, 0:2], stats[:].rearrange("p a b -> p (a b)")), [t_small], [t_small])
    P.ts('dve', sc[:, 0:1], mv[:, 1:2], eps, None, ALU.add, reads=[t_small], writes=[t_small])
    P.act(sc[:, 0:1], sc[:, 0:1], AF.Sqrt, reads=[t_small], writes=[t_small])
    P.op('dve', lambda e: e.reciprocal(sc[:, 1:2], sc[:, 0:1]), [t_small], [t_small])
    P.stt(sc[:, 2:3], mv[:, 0:1], -1.0, sc[:, 1:2], ALU.mult, ALU.mult, reads=[t_small], writes=[t_small])
    P.act(dst_hat, src, AF.Identity, scale=sc[:, 1:2], bias=sc[:, 2:3], reads=[t_src, t_small], writes=[t_dst])


def build_F(TOK):
    D, DFF = D_MODEL, D_FF
    nc = bass.Bass("TRN2", target_bir_lowering=False)
    A = NS({})
    A.x = nc.dram_tensor("x", [TOK, D], F32, kind="ExternalInput").ap()
    A.oc = nc.dram_tensor("ocat", [TOK, D], BF16, kind="ExternalInput").ap()
    A.cT = nc.dram_tensor("cT", [128, 8], F32, kind="ExternalInput").ap()
    A.wada = nc.dram_tensor("wada", [D, 4096], F32, kind="ExternalInput").ap()
    A.bada = nc.dram_tensor("bada", [1, 4096], F32, kind="ExternalInput").ap()
    A.wo = nc.dram_tensor("wo", [D, D], F32, kind="ExternalInput").ap()
    A.w1 = nc.dram_tensor("w1", [D, DFF], F32, kind="ExternalInput").ap()
    A.w2 = nc.dram_tensor("w2", [DFF, D], F32, kind="ExternalInput").ap()
    A.lnp = nc.dram_tensor("lnp", [4, D], F32, kind="ExternalInput").ap()
    A.identd = nc.dram_tensor("ident", [128, 128], BF16, kind="ExternalInput").ap()
    A.identfd = nc.dram_tensor("identf", [128, 128], F32, kind="ExternalInput").ap()
    A.xo = nc.dram_tensor("xo", [TOK, D], F32, kind="ExternalOutput").ap()
    P = Prog(nc)
    emit_F(nc, P, TOK, A)
    P.emit()
    return nc


def emit_F(nc, P, TOK, A):
    D, DFF = D_MODEL, D_FF
    NT = TOK // 128
    x, oc, cT, wada, bada, wo, w1, w2, lnp, identd, identfd, xo = (A.x, A.oc, A.cT, A.wada, A.bada, A.wo, A.w1, A.w2,
                                                                 A.lnp, A.identd, A.identfd, A.xo)
    ident = P.sb("ident_sb", [128, 128], BF16)
    identf = P.sb("identf_sb", [128, 128], F32)
    wo_bf = P.sb("wo_bf", [128, 8, D], BF16)
    w1_bf = P.sb("w1_bf", [128, 8, DFF], BF16)
    w2_bf = P.sb("w2_bf", [128, 32, D], BF16)
    NST = 2
    stage = [P.sb("stage%d" % i, [128, 1024], F32) for i in range(NST)]
    gb = P.sb("gb", [128, 2 * D], F32)
    lnb = P.sb("lnb", [128, 4, D], F32)
    scT = P.sb("scT", [128, 8], F32)
    ABt = P.sb("ABt", [128, 2, 8], F32)
    xt = P.sb("xt", [128, D], F32)
    oct_ = P.sb("oct", [128, D], BF16)
    oT = P.sb("oT", [128, 8, 128], BF16)
    r1 = P.sb("r1", [128, D], F32)
    xh = P.sb("xh", [128, D], F32)
    xhb = P.sb("xhb", [128, D], BF16)
    h2T = P.sb("h2T", [128, 8, 128], BF16)
    aT = P.sb("aT", [128, 32, 128], BF16)
    rep = aT[:, 0:16, :].rearrange("p a b -> p (a b)").bitcast(F32).rearrange("p (k j) -> p k j", k=8)
    rl = [P.sb("rl%d" % i, [128, 512], BF16) for i in range(2)]
    stats = P.sb("stats", [128, 2, 6], F32)
    mv = P.sb("mv", [128, 2], F32)
    sc = P.sb("sc", [128, 4], F32)
    ps_T = P.ps("ps_T", [128, 8, 128], BF16)
    ps_big = P.ps("ps_big", [128, D], F32)
    ps_a = [P.ps("ps_a%d" % i, [128, 512], F32) for i in range(2)]

    T = lambda n: P.tok(n)
    t_ident, t_identf = T('ident'), T('identf')
    t_wo, t_w1, t_w2 = T('wo'), T('w1'), T('w2')
    t_stage = [T('st%d' % i) for i in range(NST)]
    t_gb, t_lnb, t_scT, t_ABt = T('gb'), T('lnb'), T('scT'), T('ABt')
    t_xt, t_oct, t_oT, t_r1, t_xh, t_xhb, t_h2T, t_aT = [T(n) for n in
        ('xt', 'oct', 'oT', 'r1', 'xh', 'xhb', 'h2T', 'aT')]
    t_rep = t_aT
    t_rl = [T('rl0'), T('rl1')]
    t_small = T('small')
    t_psT, t_psbig = T('psT'), T('psbig')
    t_psa = [T('psa0'), T('psa1')]

    P.dma('sp', ident[:], identd, [], [t_ident], 'd_const')
    P.dma('sp', identf[:], identfd, [], [t_identf], 'd_const')
    P.dma('sp', scT[:], cT, [], [t_scT], 'd_const')
    for i in range(4):
        P.dma('sp', lnb[:, i, :], lnp[i:i + 1, :].to_broadcast([128, D]), [], [t_lnb], 'd_lnb')
    secdst = [(gb[:, 0:D], t_gb), (r1[:], t_r1), (xh[:], t_xh), (gb[:, D:2 * D], t_gb)]
    for sct in range(4):
        P.dma('sp', secdst[sct][0], bada[:, sct * D:(sct + 1) * D].to_broadcast([128, D]), [], [secdst[sct][1]],
              'd_modb%d' % sct)

    P.act(scT[:], scT[:], AF.Silu, reads=[t_scT], writes=[t_scT])
    P.cp('dve', rep, scT[:].unsqueeze(2).to_broadcast([128, 8, 128]), [t_scT], [t_rep])
    wada_v = wada.rearrange("(k p) n -> p k n", p=128)
    si = 0
    for jc in range(8):
        c0 = jc * 512
        pj, tpj = ps_a[jc % 2], t_psa[jc % 2]
        for kk in range(4):
            s = si % NST
            si += 1
            st = stage[s][:].rearrange("p (k n) -> p k n", k=2)
            P.dma('sp' if si % 2 == 0 else 'pool', st, wada_v[:, 2 * kk:2 * kk + 2, c0:c0 + 512], [], [t_stage[s]], 'd_st%d' % s)
            for k2 in range(2):
                k = 2 * kk + k2
                P.mm(pj[:, 0:512], rep[:, k, :], st[:, k2, :], k == 0, k == 7, [t_rep, t_stage[s]], [tpj])
        dst, tdst = secdst[c0 // D]
        cc = c0 % D
        P.tt('dve', dst[:, cc:cc + 512], pj[:, 0:512], dst[:, cc:cc + 512], ALU.add, [tpj, tdst], [tdst])
    P.ts('dve', gb[:], gb[:], 1.0, None, ALU.add, reads=[t_gb], writes=[t_gb])
    P.ts('dve', xh[:], xh[:], 1.0, None, ALU.add, reads=[t_xh], writes=[t_xh])
    P.tt('dve', xt[:], lnb[:, 0, :], xh[:], ALU.mult, [t_lnb, t_xh], [t_xt])
    P.tt('dve', xh[:], lnb[:, 1, :], xh[:], ALU.mult, [t_lnb, t_xh], [t_xh])
    P.tt('dve', xh[:], xh[:], r1[:], ALU.add, [t_xh, t_r1], [t_xh])
    idb = identf[:].unsqueeze(1).to_broadcast([128, 8, 128])
    r1v = r1[:].rearrange("p (k j) -> p k j", k=8)
    for i, (src, tsrc) in enumerate(((xt, t_xt), (xh, t_xh))):
        P.tt('dve', r1v, src[:].rearrange("p (k j) -> p k j", k=8), idb,
             ALU.mult, [tsrc, t_identf], [t_r1])
        P.op('dve', lambda e, i=i: e.reduce_sum(ABt[:, i, :], r1v, AX.X), [t_r1], [t_ABt])

    cast_engs = ['dve', 'pool', 'act']
    ci = 0

    def load_cast(dst3, src2, ncols, ttok):
        nonlocal si, ci
        K = dst3.shape[1]
        for k in range(K):
            for c0 in range(0, ncols, 1024):
                cw = min(1024, ncols - c0)
                s = si % NST
                si += 1
                P.dma('sp' if si % 2 == 0 else 'pool', stage[s][:, 0:cw], src2[k * 128:(k + 1) * 128, c0:c0 + cw],
                      [], [t_stage[s]], 'd_st%d' % s)
                P.cp(cast_engs[ci % 3], dst3[:, k, c0:c0 + cw], stage[s][:, 0:cw], [t_stage[s]], [ttok])
                ci += 1

    load_cast(wo_bf, wo, D, t_wo)
    load_cast(w1_bf, w1, DFF, t_w1)
    load_cast(w2_bf, w2, D, t_w2)

    xh2 = stage[0]
    st1b = stage[1][:].bitcast(BF16)
    xhs = [xh, xh2]
    t_xhs = [t_xh, t_stage[0]]
    xhbs = [xhb[:], st1b[:, 0:1024]]
    t_xhbs = [t_xhb, t_stage[1]]
    octs = [oct_[:], st1b[:, 1024:2048]]
    t_octs = [t_oct, T('oct2')]
    ps_T2 = P.ps("ps_T2", [128, 8, 128], BF16)
    ps_big2 = P.ps("ps_big2", [128, D], F32)
    t_psT2, t_psbig2 = T('psT2'), T('psbig2')

    def stageA(t):
        s = t % 2
        rows = slice(t * 128, (t + 1) * 128)
        xhc, t_xhc = xhs[s], t_xhs[s]
        P.dma('sp', xt[:], x[rows, :], [], [t_xt], 'd_xt')
        for k in range(8):
            P.tr(ps_T[:, k, :], octs[s][:, k * 128:(k + 1) * 128], ident[:], [t_octs[s], t_ident], [t_psT])
        P.cp('act', oT[:], ps_T[:], [t_psT], [t_oT])
        for n in range(2):
            for k in range(8):
                P.mm(ps_big[:, n * 512:(n + 1) * 512], oT[:, k, :], wo_bf[:, k, n * 512:(n + 1) * 512],
                     k == 0, k == 7, [t_oT, t_wo], [t_psbig])

        def chain():
            P.tt('dve', r1[:], ps_big[:], gb[:, 0:D], ALU.mult, [t_psbig, t_gb], [t_r1])
            P.stt(r1[:], xt[:], DN_ALPHA, r1[:], ALU.mult, ALU.add, reads=[t_xt, t_r1], writes=[t_r1])
            ln_tile(P, r1[:], xhc[:], stats, mv, sc, 1e-5, t_r1, t_xhc, t_small)
        P.atomic(chain)
        P.cp('act', xhbs[s], xhc[:], [t_xhc], [t_xhbs[s]])
        P.tt('dve', xhc[:], xhc[:], lnb[:, 0, :], ALU.mult, [t_xhc, t_lnb], [t_xhc])
        P.tt('dve', xhc[:], xhc[:], lnb[:, 1, :], ALU.add, [t_xhc, t_lnb], [t_xhc])

    def stageB(t):
        s = t % 2
        rows = slice(t * 128, (t + 1) * 128)
        xhc, t_xhc = xhs[s], t_xhs[s]
        for k in range(8):
            P.tr(ps_T2[:, k, :], xhbs[s][:, k * 128:(k + 1) * 128], ident[:], [t_xhbs[s], t_ident], [t_psT2])
        for k in range(8):
            P.ts('dve', h2T[:, k, :], ps_T2[:, k, :], ABt[:, 0, k:k + 1], ABt[:, 1, k:k + 1], ALU.mult, ALU.add,
                 reads=[t_psT2, t_ABt], writes=[t_h2T])
        for fg in range(8):
            pa = fg % 2
            for fi in range(4):
                f = fg * 4 + fi
                for k in range(8):
                    P.mm(ps_a[pa][:, fi * 128:(fi + 1) * 128], w1_bf[:, k, f * 128:(f + 1) * 128], h2T[:, k, :],
                         k == 0, k == 7, [t_w1, t_h2T], [t_psa[pa]])
            P.act(rl[pa][:], ps_a[pa][:], AF.Relu, reads=[t_psa[pa]], writes=[t_rl[pa]])
            P.tt('pool', aT[:, fg * 4:(fg + 1) * 4, :].rearrange("p a b -> p (a b)"), rl[pa][:], rl[pa][:], ALU.mult,
                 [t_rl[pa]], [t_aT])
        for n in range(2):
            for f in range(32):
                P.mm(ps_big2[:, n * 512:(n + 1) * 512], aT[:, f, :], w2_bf[:, f, n * 512:(n + 1) * 512],
                     f == 0, f == 31, [t_aT, t_w2], [t_psbig2])

    def stageB2(t):
        s = t % 2
        rows = slice(t * 128, (t + 1) * 128)
        xhc, t_xhc = xhs[s], t_xhs[s]

        def chain():
            P.tt('dve', r1[:], ps_big2[:], gb[:, D:2 * D], ALU.mult, [t_psbig2, t_gb], [t_r1])
            P.stt(r1[:], xhc[:], DN_ALPHA, r1[:], ALU.mult, ALU.add, reads=[t_xhc, t_r1], writes=[t_r1])
            ln_tile(P, r1[:], xhc[:], stats, mv, sc, 1e-5, t_r1, t_xhc, t_small)
        P.atomic(chain)
        P.tt('dve', xhc[:], xhc[:], lnb[:, 2, :], ALU.mult, [t_xhc, t_lnb], [t_xhc])
        P.tt('dve', xhc[:], xhc[:], lnb[:, 3, :], ALU.add, [t_xhc, t_lnb], [t_xhc])
        P.dma('sp', xo[rows, :], xhc[:], [t_xhc], [], 'd_out')

    def load_oct(t):
        P.dma('pool', octs[t % 2], oc[t * 128:(t + 1) * 128, :], [], [t_octs[t % 2]], 'd_oct%d' % (t % 2))

    load_oct(0)
    stageA(0)
    if NT > 1:
        load_oct(1)
    for t in range(NT):
        if t + 2 < NT:
            load_oct(t + 2)

        def side():
            if t > 0:
                stageB2(t - 1)
            if t + 1 < NT:
                stageA(t + 1)
        P.replay(P.record(lambda: stageB(t)), P.record(side), speeds=[1.0, FSPEED])
    stageB2(NT - 1)


def f_inputs(l, b, tok_slice, x_full, ocat_b, inp, consts):
    return {
        "x": np.ascontiguousarray(x_full[b, tok_slice, :]),
        "ocat": np.ascontiguousarray(ocat_b[tok_slice, :]),
        "cT": np.ascontiguousarray(inp['c'][b].reshape(8, 128).T),
        "wada": np.ascontiguousarray(inp['w_ada'][l][:, 2048:6144]),
        "bada": np.ascontiguousarray(inp['b_ada'][l][None, 2048:6144]),
        "wo": inp['w_o'][l], "w1": inp['w_ff1'][l], "w2": inp['w_ff2'][l],
        "lnp": np.stack([inp['ln1_w'][l], inp['ln1_b'][l], inp['ln2_w'][l], inp['ln2_b'][l]]),
        "ident": consts['ident'], "identf": consts['identf'],
    }


NEGB = -30000.0
SKIP = set()
GRAN = 1
FSPEED = 1.0
NCOL = 1548
HEAD_DIM = 64


def p_consts(S):
    c = {}
    c['ident'] = np.eye(128, dtype=np.float32).astype(NPBF)
    c['identf'] = np.eye(128, dtype=np.float32)
    s = np.arange(64)
    tri = (s[:, None] <= s[None, :]).astype(np.float32)
    c['tri64'] = np.ascontiguousarray(np.broadcast_to(tri[:, None, :], (64, 2, 64))).astype(np.float32)
    rm = np.ones((64, 512), np.float32)
    rm[:, ::64] = 0
    c['resetm'] = rm
    s = np.arange(128)
    c['tri128'] = (s[:, None] <= s[None, :]).astype(np.float32)
    causal = np.where(s[:, None] <= s[None, :], 0.0, NEGB).astype(np.float32)
    upper = np.where(s[:, None] > s[None, :], 0.0, NEGB).astype(np.float32)
    c['causal4'] = np.ascontiguousarray(np.broadcast_to(causal[:, None, :], (128, 4, 128))).astype(NPBF)
    c['upper4'] = np.ascontiguousarray(np.broadcast_to(upper[:, None, :], (128, 4, 128))).astype(NPBF)
    cb = np.zeros((128, 17, 128), np.float32)
    m = np.arange(128)[:, None]
    i = np.arange(128)[None, :]
    for a in range(16):
        cb[:, a, :] = np.where(i >= 16 * (m - 8 * a) + 31, 0.0, NEGB)
    cb[:, 16, :] = np.where((m == 127) & (i < 15), NEGB, 0.0)
    c['cbv'] = cb.astype(NPBF)
    key = np.arange(S)
    c['eall'] = (key[None, :] // 64 == np.arange(128)[:, None]).astype(np.float32).astype(NPBF)
    c['ehalf'] = ((key[None, :] // 64) % 64 == np.arange(64)[:, None]).astype(np.float32).astype(NPBF)
    nC = S // 16 - 1
    nS = S // 64
    cs = np.arange(nC) * 16
    ss = np.arange(nS) * 64
    ov = ((cs[:, None] < ss[None, :] + 64) & (ss[None, :] < cs[:, None] + 32)).astype(np.float32)
    OV = np.zeros((512, 128), np.float32)
    OV[:nC, :nS] = ov
    c['ov'] = np.ascontiguousarray(OV.reshape(4, 128, 128).transpose(1, 0, 2)).astype(NPBF)
    ii = np.arange(128)[:, None]
    jp = np.arange(256)[None, :] - 127
    F = np.zeros((128, 256), np.float32)
    F = np.where(jp > 1, -(100.0 + jp), F)
    F = np.where(jp == 1, np.where(ii < 64, -101.0, 110.0), F)
    F = np.where(jp == 0, np.where(ii < 64, 110.0, 120.0), F)
    F = np.where(jp == -1, np.where(ii < 64, 120.0, 0.0), F)
    c['fbig'] = F.astype(np.float32)
    invf = (10000.0 ** (-np.arange(0, 64, 2, dtype=np.float32) / 64)).astype(np.float32)
    c['invf'] = np.ascontiguousarray(np.broadcast_to(invf[None, :], (128, 32))).astype(np.float32)
    return c


def gelu_tanh(P, dst, x, tmp, eng, tx, tdst, ttmp):
    P.tt(eng, tmp, x, x, ALU.mult, [tx], [ttmp])
    P.ts(eng, tmp, tmp, 0.044715, 1.0, ALU.mult, ALU.add, reads=[ttmp], writes=[ttmp])
    P.tt(eng, tmp, tmp, x, ALU.mult, [ttmp, tx], [ttmp])
    P.act(tmp, tmp, AF.Sigmoid, scale=1.5957691216057308, reads=[ttmp], writes=[ttmp])
    P.tt(eng, dst, tmp, x, ALU.mult, [ttmp, tx], [tdst])


def build_P(S, stages=(1, 2, 3)):
    D = D_MODEL
    NT = S // 128
    nc = bass.Bass("TRN2", target_bir_lowering=False)

    def din(name, shape, dt=F32):
        return nc.dram_tensor(name, list(shape), dt, kind="ExternalInput").ap()

    A = NS({})
    A.x = din("x", [S, D])
    A.cT = din("cT", [128, 8])
    A.wada = din("wada", [D, 2048])
    A.badaT = din("badaT", [128, 16])
    A.win = din("win", [D, NCOL])
    A.posT = din("posT", [128, NT], I32)
    A.lbT = din("lbT", [64, 2, 4])
    A.lsel = din("lsel", [64, 2, 4])
    A.hnw = din("hnw", [1, 64])
    A.pek = din("pekT", [64, 32])
    A.pev = din("pevT", [64, 32])
    A.w1k = din("w1k", [2048, 256])
    A.w1v = din("w1v", [2048, 256])
    A.w2k = din("w2k", [256, 64])
    A.w2v = din("w2v", [256, 64])
    A.gnw = din("gnw", [1, 256])
    A.gnb = din("gnb", [1, 256])
    A.wsT = din("wsT", [2, 128, 128])
    A.bsT = din("bsT", [128, 2])
    A.c_ident = din("ident", [128, 128], BF16)
    A.c_identf = din("identf", [128, 128])
    A.c_tri64 = din("tri64", [64, 2, 64])
    A.c_resetm = din("resetm", [64, 512])
    A.c_tri128 = din("tri128", [128, 128])
    A.c_causal4 = din("causal4", [128, 4, 128], BF16)
    A.c_upper4 = din("upper4", [128, 4, 128], BF16)
    A.c_cbv = din("cbv", [128, 17, 128], BF16)
    A.c_ehalf = din("ehalf", [64, S], BF16)
    A.c_ov = din("ov", [128, 4, 128], BF16)
    A.c_fbig = din("fbig", [128, 256])
    A.c_invf = din("invf", [128, 32])
    o = nc.dram_tensor("o", [S, 512], BF16, kind="ExternalOutput").ap()
    A.o_h, A.o_n, A.o_g = o[:, 0:128], o[:, 128:384], o[:, 384:512]
    A.qTd = nc.dram_tensor("qTd", [NT, 64, 512], BF16).ap()
    P = Prog(nc)
    emit_P(nc, P, S, A, stages)
    P.emit()
    return nc


def emit_P(nc, P, S, A, stages=(1, 2, 3)):
    D = D_MODEL
    NT = S // 128
    NB = S // 512
    nC = S // 16 - 1
    x = A.x
    cT = A.cT
    wada = A.wada
    badaT = A.badaT
    win = A.win
    posT = A.posT
    lbT = A.lbT
    lsel = A.lsel
    hnw = A.hnw
    pek = A.pek
    pev = A.pev
    w1k = A.w1k
    w1v = A.w1v
    w2k = A.w2k
    w2v = A.w2v
    gnw = A.gnw
    gnb = A.gnb
    wsT = A.wsT
    bsT = A.bsT
    c_ident = A.c_ident
    c_identf = A.c_identf
    c_tri64 = A.c_tri64
    c_resetm = A.c_resetm
    c_tri128 = A.c_tri128
    c_causal4 = A.c_causal4
    c_upper4 = A.c_upper4
    c_cbv = A.c_cbv
    c_ehalf = A.c_ehalf
    c_ov = A.c_ov
    c_fbig = A.c_fbig
    c_invf = A.c_invf
    o_h, o_n, o_g, qTd = A.o_h, A.o_n, A.o_g, A.qTd
    T = P.tok
    ident = P.sb("ident_sb", [128, 128], BF16)
    identf = P.sb("identf_sb", [128, 128], F32)
    win_bf = P.sb("win_bf", [128, 8, NCOL], BF16)
    kT = P.sb("kT", [128, 4, S], BF16)
    vsl = P.sb("vsl", [128, NT, 65], BF16)
    vwn = P.sb("vwn", [128, NT, 65], BF16)
    gates = P.sb("gates", [128, NT, 12], F32)
    cs = P.sb("cs", [128, NT, 64], F32)
    modT = P.sb("modT", [128, 16], F32)
    t_ident, t_identf, t_win, t_kT, t_vsl, t_vwn, t_gates, t_cs, t_modT = [T(n) for n in
        ('ident', 'identf', 'win', 'kT', 'vsl', 'vwn', 'gates', 'cs', 'modT')]
    P.dma('sp', ident[:], c_ident, [], [t_ident], 'd_c0')
    P.dma('sp', identf[:], c_identf, [], [t_identf], 'd_c0')
    P.dma('sp', kT[64:128, 1, :], c_ehalf, [], [t_kT], 'd_c0')
    P.op('pool', lambda e: e.memset(kT[64:128, 2, :], 0.0), [], [t_kT])

    es1 = ExitStack()
    es0 = ExitStack()
    sb0 = lambda n, sh, dt: es0.enter_context(nc.sbuf_tensor(P.pfx + "a0_" + n, list(sh), dt))
    sb1 = lambda n, sh, dt: es1.enter_context(nc.sbuf_tensor(P.pfx + "a1_" + n, list(sh), dt))
    ps1 = lambda n, sh, dt: es1.enter_context(nc.psum_tensor(P.pfx + "q1_" + n, list(sh), dt))
    lb = P.sb("lb", [64, 2], F32)
    oml = P.sb("oml", [64, 2], F32)
    noml = P.sb("noml", [64, 2], F32)
    tri64 = P.sb("tri64", [64, 2, 64], F32)
    resetm = P.sb("resetm", [64, 512], F32)
    tri128 = P.sb("tri128", [128, 128], F32)
    hnwb = P.sb("hnwb", [64, 2, 64], F32)
    gnwb = P.sb("gnwb", [128, 256], F32)
    gnbb = P.sb("gnbb", [128, 256], F32)
    ws_bf = P.sb("ws_bf", [128, 2, 128], BF16)
    bs = P.sb("bs", [128, 2], F32)
    stage = [sb0("stage%d" % i, [128, NCOL], F32) for i in range(2)]
    t_stage = [T('st0'), T('st1')]
    scT = sb0("scT", [128, 8], F32)
    t_scT = T('scT')
    pA = [ps1("pA%d" % i, [128, 512], F32) for i in range(3)]
    t_pA = [T('pA%d' % i) for i in range(3)]
    pT = ps1("pT", [128, 1024], BF16)
    t_pT = T('pT')
    pTf = ps1("pTf", [128, 512], F32)
    t_pTf = T('pTf')
    pH_au = ps1("pH_au", [128, 512], F32)
    pH_attn = pH_au[0:64, 0:128].rearrange("p (h t) -> p h t", h=2)
    pH_U = pH_au[0:64, 128:256].rearrange("p (h t) -> p h t", h=2)
    pH_o = ps1("pH_o", [128, 512], F32)[0:64, 0:128]
    t_pHa, t_pHo = T('pHa'), T('pHo')
    t_pHU = t_pHa
    pT2 = ps1("pT2", [128, 1024], BF16)
    t_pT2 = T('pT2')
    pai = {'h': 0, 't': 0}
    pools = {'h': [0], 't': [1, 2]}

    def nextpA(stream='t'):
        pl = pools[stream]
        i = pl[pai[stream] % len(pl)]
        pai[stream] += 1
        return pA[i], t_pA[i]

    P.dma('sp', scT[:], cT, [], [t_scT], 'd_c1')
    P.dma('sp', modT[:], badaT, [], [t_modT], 'd_c1')
    P.act(scT[:], scT[:], AF.Silu, reads=[t_scT], writes=[t_scT])
    wada_v = wada.rearrange("(k p) n -> p k n", p=128)
    for j in range(16):
        s = j % 2
        st = stage[s][:, 0:1024].rearrange("p (k n) -> p k n", k=8)
        P.dma('sp', st, wada_v[:, :, j * 128:(j + 1) * 128], [], [t_stage[s]], 'd_st%d' % s)
        ps_, tps_ = nextpA()
        for k in range(8):
            P.mm(ps_[:, 0:1], st[:, k, :], scT[:, k:k + 1], k == 0, k == 7, [t_stage[s], t_scT], [tps_])
        P.tt('dve', modT[:, j:j + 1], ps_[:, 0:1], modT[:, j:j + 1], ALU.add, [tps_, t_modT], [t_modT])
    P.ts('dve', modT[:, 8:16], modT[:, 8:16], 1.0, None, ALU.add, reads=[t_modT], writes=[t_modT])

    for k in range(8):
        s = k % 2
        P.dma('sp' if s == 0 else 'pool', stage[s][:], win[k * 128:(k + 1) * 128, :], [], [t_stage[s]], 'd_st%d' % s)
        P.cp(['dve', 'pool'][k % 2], win_bf[:, k, :], stage[s][:], [t_stage[s]], [t_win])

    posi = sb0("posi", [128, NT], I32)
    posf = sb0("posf", [128, NT], F32)
    invf = sb0("invf", [128, 32], F32)
    ang = sb0("ang", [128, NT, 32], F32)
    rr = sb0("rr", [128, NT, 32], F32)
    ki = sb0("ki", [128, NT, 32], I32)
    kf = sb0("kf", [128, NT, 32], F32)
    t_rope = T('rope')
    P.dma('sp', posi[:], posT, [], [t_rope], 'd_c2')
    P.dma('sp', invf[:], c_invf, [], [t_rope], 'd_c2')
    P.cp('dve', posf[:], posi[:], [t_rope], [t_rope])
    P.tt('dve', ang[:], posf[:].unsqueeze(2).to_broadcast([128, NT, 32]),
         invf[:].unsqueeze(1).to_broadcast([128, NT, 32]), ALU.mult, [t_rope], [t_rope])
    TWO_PI = 2.0 * np.pi
    C1 = 6.28125
    C2 = TWO_PI - C1
    for which in (1, 0):
        src = ang
        if which == 0:
            P.ts('dve', rr[:], ang[:], float(np.pi / 2), None, ALU.add, reads=[t_rope], writes=[t_rope])
            src = rr
        P.ts('dve', kf[:], src[:], float(1.0 / TWO_PI), None, ALU.mult, reads=[t_rope], writes=[t_rope])
        P.cp('dve', ki[:], kf[:], [t_rope], [t_rope])
        P.cp('dve', kf[:], ki[:], [t_rope], [t_rope])
        P.stt(rr[:], kf[:], -C1, src[:], ALU.mult, ALU.add, reads=[t_rope], writes=[t_rope])
        P.stt(rr[:], kf[:], -C2, rr[:], ALU.mult, ALU.add, reads=[t_rope], writes=[t_rope])
        P.ts('dve', kf[:], rr[:], float(np.pi), float(-TWO_PI), ALU.is_gt, ALU.mult, reads=[t_rope], writes=[t_rope])
        P.tt('dve', rr[:], rr[:], kf[:], ALU.add, [t_rope], [t_rope])
        P.ts('dve', kf[:], rr[:], float(-np.pi), float(TWO_PI), ALU.is_lt, ALU.mult, reads=[t_rope], writes=[t_rope])
        P.tt('dve', rr[:], rr[:], kf[:], ALU.add, [t_rope], [t_rope])
        P.act(cs[:, :, which * 32:(which + 1) * 32], rr[:], AF.Sin, reads=[t_rope], writes=[t_cs])

    lbr = sb0("lbr", [64, 2, 4], F32)
    lsl = sb0("lsl", [64, 2, 4], F32)
    lbs = sb0("lbs", [64, 2], F32)
    t_lb = T('lb')
    P.dma('sp', lbr[:], lbT, [], [t_lb], 'd_c3')
    P.dma('sp', lsl[:], lsel, [], [t_lb], 'd_c3')
    P.act(lbr[:], lbr[:], AF.Exp, reads=[t_lb], writes=[t_lb])
    P.op('dve', lambda e: e.reduce_sum(lbs[:], lbr[:], AX.X), [t_lb], [t_lb])
    P.op('dve', lambda e: e.reciprocal(lbs[:], lbs[:]), [t_lb], [t_lb])
    P.tt('dve', lbr[:], lbr[:], lsl[:], ALU.mult, [t_lb], [t_lb])
    P.op('dve', lambda e: e.reduce_sum(lb[:], lbr[:], AX.X), [t_lb], [t_lb])
    P.tt('dve', lb[:], lb[:], lbs[:], ALU.mult, [t_lb], [t_lb])
    P.ts('dve', oml[:], lb[:], -1.0, 1.0, ALU.mult, ALU.add, reads=[t_lb], writes=[t_lb])
    P.ts('dve', noml[:], oml[:], -1.0, None, ALU.mult, reads=[t_lb], writes=[t_lb])

    wsf = sb0("wsf", [128, 2, 128], F32)
    t_c1 = T('c1')
    P.dma('sp', tri64[:], c_tri64, [], [t_c1], 'd_c4')
    P.dma('sp', resetm[:], c_resetm, [], [t_c1], 'd_c4')
    P.dma('sp', tri128[:], c_tri128, [], [t_c1], 'd_c4')
    for h in range(2):
        P.dma('sp', hnwb[:, h, :], hnw.to_broadcast([64, 64]), [], [t_c1], 'd_c4')
        P.dma('sp', wsf[:, h, :], wsT[h], [], [t_c1], 'd_c4')
    P.dma('sp', gnwb[:], gnw.to_broadcast([128, 256]), [], [t_c1], 'd_c4')
    P.dma('sp', gnbb[:], gnb.to_broadcast([128, 256]), [], [t_c1], 'd_c4')
    P.dma('sp', bs[:], bsT, [], [t_c1], 'd_c4')
    P.tt('dve', ws_bf[:], wsf[:], tri128[:].unsqueeze(1).to_broadcast([128, 2, 128]), ALU.mult, [t_c1], [t_c1])
    P.op('pool', lambda e: e.memset(vsl[:, :, 64:65], 1.0), [], [t_vsl])
    P.op('pool', lambda e: e.memset(vwn[:, :, 64:65], 1.0), [], [t_vwn])

    P.barrier()
    es0.close()
    xt = [sb1("xt%d" % i, [128, D], F32) for i in range(2)]
    t_xt = [T('xt0'), T('xt1')]
    hTs = [sb1("hT%d" % i, [128, 8, 512], BF16) for i in range(2)]
    t_hTs = [T('hT0'), T('hT1')]
    hq, hsg, hlf, hb, hkk, he = [sb1(n, [64, 2, 512], F32) for n in ('hq', 'hsg', 'hlf', 'hb', 'hkk', 'he')]
    t_hq, t_hsg, t_hlf, t_hb, t_hkk, t_he = [T(n) for n in ('hq', 'hsg', 'hlf', 'hb', 'hkk', 'he')]
    AT, BT, CqT, CkT, iTb, sgT = [sb1(n, [64, 2, 512], BF16) for n in ('AT', 'BT', 'CqT', 'CkT', 'iTb', 'sgT')]
    t_AT, t_BT, t_CqT, t_CkT, t_iTb, t_sgT = [T(n) for n in ('AT', 'BT', 'CqT', 'CkT', 'iTb', 'sgT')]
    dcol = sb1("dcol", [64, 2, 8], F32)
    t_dcol = T('dcol')
    Sst = sb1("Sst", [64, 2, 64], F32)
    S_bf = sb1("S_bf", [64, 2, 64], BF16)
    t_S, t_Sbf = T('S'), T('Sbf')
    tok3 = sb1("tok3", [64, 6, 64], BF16)
    t_tok3 = T('tok3')
    attn_bf = sb1("attn_bf", [64, 2, 64], BF16)
    t_attn = T('attn')
    osb = sb1("osb", [64, 128], F32)
    osq = sb1("osq", [64, 128], F32)
    oss = sb1("oss", [64, 4], F32)
    ohb = sb1("ohb", [64, 128], BF16)
    t_osb, t_osq, t_oss, t_ohb = T('osb'), T('osq'), T('oss'), T('ohb')
    rt1 = sb1("rt1", [128, 7, 32], F32)
    rt2 = sb1("rt2", [128, 7, 32], F32)
    qk_tm = sb1("qk_tm", [128, 8, 64], BF16)
    t_rt1, t_rt2, t_qk = T('rt1'), T('rt2'), T('qk')
    qTt = [sb1("qTt%d" % i, [64, 512], BF16) for i in range(2)]
    t_qTt = [T('qTt0'), T('qTt1')]
    gx = sb1("gx", [128, 384], F32)
    gtmp = sb1("gtmp", [128, 384], F32)
    gg = sb1("gg", [128, 384], F32)
    gvh = sb1("gvh", [128, 256], F32)
    gvn = sb1("gvn", [128, 128], BF16)
    ogb = sb1("ogb", [128, 128], BF16)
    gstats = sb1("gstats", [128, 6], F32)
    gmv = sb1("gmv", [128, 2], F32)
    gsc = sb1("gsc", [128, 4], F32)
    t_gx, t_gtmp, t_gg, t_gvh, t_gvn, t_ogb, t_gsm = [T(n) for n in ('gx', 'gtmp', 'gg', 'gvh', 'gvn', 'ogb', 'gsm')]

    P.op('dve', lambda e: e.memset(Sst[:], 0.0), [], [t_S])
    P.op('dve', lambda e: e.memset(S_bf[:], 0.0), [], [t_Sbf])

    def part_hT(blk):
        hT, t_hT = hTs[blk % 2], t_hTs[blk % 2]
        for ti in range(4):
            t = blk * 4 + ti
            xb = xt[t % 2]
            txb = t_xt[t % 2]
            P.dma('sp' if t % 2 == 0 else 'pool', xb[:], x[t * 128:(t + 1) * 128, :], [], [txb], 'd_xt%d' % (t % 2))
            for kh in range(2):
                for k4 in range(4):
                    k = kh * 4 + k4
                    P.tr(pTf[:, k4 * 128:(k4 + 1) * 128], xb[:, k * 128:(k + 1) * 128], identf[:], [txb, t_identf], [t_pTf])
                for k4 in range(4):
                    k = kh * 4 + k4
                    P.act(hT[:, k, ti * 128:(ti + 1) * 128], pTf[:, k4 * 128:(k4 + 1) * 128], AF.Identity,
                          scale=modT[:, 8 + k:9 + k], bias=modT[:, k:k + 1], reads=[t_pTf, t_modT], writes=[t_hT])

    def part_hgrn(blk):
        hT, t_hT = hTs[blk % 2], t_hTs[blk % 2]
        for h in (range(2) if 'hgrn' not in SKIP else ()):
            def proj(qi):
                ps_, tps_ = nextpA('h')
                c0 = qi * 128 + h * 64
                for k in range(8):
                    P.mm(ps_[0:64, :], win_bf[:, k, c0:c0 + 64], hT[:, k, :], k == 0, k == 7, [t_win, t_hT], [tps_])
                return ps_, tps_
            ps_, tps_ = proj(0)
            P.act(hq[:, h, :], ps_[0:64, :], AF.Silu, reads=[tps_], writes=[t_hq])
            ps_, tps_ = proj(3)
            P.act(sgT[:, h, :], ps_[0:64, :], AF.Silu, reads=[tps_], writes=[t_sgT])
            ps_, tps_ = proj(1)
            P.act(hsg[:, h, :], ps_[0:64, :], AF.Sigmoid, reads=[tps_], writes=[t_hsg])
            ps_, tps_ = proj(2)
            P.cp('dve', iTb[:, h, :], ps_[0:64, :], [tps_], [t_iTb])
            P.ts('dve', hlf[:, h, :], hsg[:, h, :], oml[:, h:h + 1], lb[:, h:h + 1], ALU.mult, ALU.add,
                 reads=[t_hsg, t_lb], writes=[t_hlf])
            P.ts('dve', hlf[:, h, :], hlf[:, h, :], 1e-30, None, ALU.max, reads=[t_hlf], writes=[t_hlf])
            P.ts('dve', hkk[:, h, :], hsg[:, h, :], noml[:, h:h + 1], oml[:, h:h + 1], ALU.mult, ALU.add,
                 reads=[t_hsg, t_lb], writes=[t_hkk])
        if 'hgrn' not in SKIP:
            P.act(hlf[:], hlf[:], AF.Ln, reads=[t_hlf], writes=[t_hlf])
            for h in range(2):
                P.op('dve', lambda e, h=h: e.tensor_tensor_scan(hb[:, h, :], resetm[:], hlf[:, h, :], 0.0, ALU.mult, ALU.add),
                     [t_hlf, t_c1], [t_hb])
            hbv = hb[:].rearrange("p h (c t) -> p (h c) t", t=64)
            hev = he[:].rearrange("p h (c t) -> p (h c) t", t=64)
            rmid = hbv[:, :, 31:32].to_broadcast([64, 16, 64])
            rlast = hbv[:, :, 63:64].to_broadcast([64, 16, 64])
            he2 = he[:].rearrange("p h t -> p (h t)")
            hb2 = hb[:].rearrange("p h t -> p (h t)")
            hq2 = hq[:].rearrange("p h t -> p (h t)")
            hkk2 = hkk[:].rearrange("p h t -> p (h t)")
            f2 = lambda a: a[:].rearrange("p h t -> p (h t)")
            P.tt('dve', hev, hbv, rmid, ALU.subtract, [t_hb], [t_he])
            P.ts('dve', he2, he2, 43.0, None, ALU.min, reads=[t_he], writes=[t_he])
            P.act(he2, he2, AF.Exp, reads=[t_he], writes=[t_he])
            P.tt('dve', f2(AT), he2, hq2, ALU.mult, [t_he, t_hq], [t_AT])
            P.tt('dve', hev, rmid, hbv, ALU.subtract, [t_hb], [t_he])
            P.ts('dve', he2, he2, 43.0, None, ALU.min, reads=[t_he], writes=[t_he])
            P.act(he2, he2, AF.Exp, reads=[t_he], writes=[t_he])
            P.tt('dve', f2(BT), he2, hkk2, ALU.mult, [t_he, t_hkk], [t_BT])
            P.act(he2, hb2, AF.Exp, reads=[t_hb], writes=[t_he])
            P.tt('dve', f2(CqT), he2, hq2, ALU.mult, [t_he, t_hq], [t_CqT])
            P.tt('dve', hev, rlast, hbv, ALU.subtract, [t_hb], [t_he])
            P.act(he2, he2, AF.Exp, reads=[t_he], writes=[t_he])
            P.tt('dve', f2(CkT), he2, hkk2, ALU.mult, [t_he, t_hkk], [t_CkT])
            P.act(dcol[:].rearrange("p h c -> p (h c)"), hbv[:, :, 63], AF.Exp, reads=[t_hb], writes=[t_dcol])

        for c in (range(8) if ('hgrn' not in SKIP and 'hchunks' not in SKIP) else ()):
            csl = slice(c * 64, (c + 1) * 64)
            row0 = blk * 512 + c * 64
            for h in range(2):
                P.tr(pT[0:64, (0 + h) * 64:(1 + h) * 64], iTb[:, h, csl], ident[0:64, 0:64], [t_iTb, t_ident], [t_pT])
                P.tr(pT[0:64, (2 + h) * 64:(3 + h) * 64], CkT[:, h, csl], ident[0:64, 0:64], [t_CkT, t_ident], [t_pT])
                P.tr(pT[0:64, (4 + h) * 64:(5 + h) * 64], sgT[:, h, csl], ident[0:64, 0:64], [t_sgT, t_ident], [t_pT])
            P.cp('act', tok3[:].rearrange("p a b -> p (a b)"), pT[0:64, 0:384], [t_pT], [t_tok3])
            for h in range(2):
                P.mm(pH_attn[:, h, :], BT[:, h, csl], AT[:, h, csl], True, True, [t_BT, t_AT], [t_pHa])
            P.tt('dve', attn_bf[:], pH_attn[:], tri64[:], ALU.mult, [t_pHa, t_c1], [t_attn])
            for h in range(2):
                P.mm(pH_o[:, h * 64:(h + 1) * 64], attn_bf[:, h, :], tok3[:, h, :], True, False, [t_attn, t_tok3], [t_pHo])
                P.mm(pH_o[:, h * 64:(h + 1) * 64], CqT[:, h, csl], S_bf[:, h, :], False, True, [t_CqT, t_Sbf], [t_pHo])
            for h in range(2):
                P.mm(pH_U[:, h, :], tok3[:, 2 + h, :], tok3[:, h, :], True, True, [t_tok3], [t_pHU])
            for h in range(2):
                P.stt(Sst[:, h, :], Sst[:, h, :], dcol[:, h, c:c + 1], pH_U[:, h, :], ALU.mult, ALU.add,
                      reads=[t_S, t_dcol, t_pHU], writes=[t_S])
            P.cp('pool', S_bf[:], Sst[:], [t_S], [t_Sbf])
            P.cp('act', osb[:], pH_o[:], [t_pHo], [t_osb])
            P.tt('dve', osq[:], osb[:], osb[:], ALU.mult, [t_osb], [t_osq])
            P.op('dve', lambda e: e.reduce_sum(oss[:, 0:2], osq[:].rearrange("p (h v) -> p h v", h=2), AX.X), [t_osq], [t_oss])
            P.ts('dve', oss[:, 0:2], oss[:, 0:2], 1.0 / 64, 1e-6, ALU.mult, ALU.add, reads=[t_oss], writes=[t_oss])
            P.act(oss[:, 0:2], oss[:, 0:2], AF.Sqrt, reads=[t_oss], writes=[t_oss])
            P.op('dve', lambda e: e.reciprocal(oss[:, 2:4], oss[:, 0:2]), [t_oss], [t_oss])
            o3 = osb[:].rearrange("p (h v) -> p h v", h=2)
            P.tt('dve', o3, o3, oss[:, 2:4].unsqueeze(2).to_broadcast([64, 2, 64]), ALU.mult, [t_osb, t_oss], [t_osb])
            P.tt('dve', o3, o3, hnwb[:], ALU.mult, [t_osb, t_c1], [t_osb])
            P.tt('dve', ohb[:].rearrange("p (h v) -> p h v", h=2), o3, tok3[:, 4:6, :], ALU.mult, [t_osb, t_tok3], [t_ohb])
            P.dma('sp', o_h[row0:row0 + 64, :], ohb[:], [t_ohb], [], 'd_oh')


    def part_tok(blk):
        hT, t_hT = hTs[blk % 2], t_hTs[blk % 2]
        for ti in (range(4) if 'tok' not in SKIP else ()):
            t = blk * 4 + ti
            tsl = slice(ti * 128, (ti + 1) * 128)
            rows = slice(t * 128, (t + 1) * 128)
            if 'nsa_tm' not in SKIP:
                psB, tpsB = nextpA()
                for k in range(8):
                    P.mm(psB[:], hT[:, k, tsl], win_bf[:, k, 512:1024], k == 0, k == 7, [t_hT, t_win], [tpsB])
                B4 = psB[:, 0:448].rearrange("p (a two d) -> p a two d", two=2, d=32)
                a1 = B4[:, :, 0, :]
                a2 = B4[:, :, 1, :]
                cosb = cs[:, t, 0:32].unsqueeze(1).to_broadcast([128, 7, 32])
                sinb = cs[:, t, 32:64].unsqueeze(1).to_broadcast([128, 7, 32])
                Q4 = qk_tm[:, 0:7, :].rearrange("p a (two d) -> p a two d", two=2)
                P.tt('dve', rt1[:], a1, cosb, ALU.mult, [tpsB, t_cs], [t_rt1])
                P.tt('dve', rt2[:], a2, sinb, ALU.mult, [tpsB, t_cs], [t_rt2])
                P.tt('dve', Q4[:, :, 0, :], rt1[:], rt2[:], ALU.subtract, [t_rt1, t_rt2], [t_qk])
                P.tt('dve', rt1[:], a2, cosb, ALU.mult, [tpsB, t_cs], [t_rt1])
                P.tt('dve', rt2[:], a1, sinb, ALU.mult, [tpsB, t_cs], [t_rt2])
                P.tt('dve', Q4[:, :, 1, :], rt1[:], rt2[:], ALU.add, [t_rt1, t_rt2], [t_qk])
                P.cp('act', qk_tm[:, 7, :], psB[:, 448:512], [tpsB], [t_qk])
                if 'cut1' not in SKIP:
                    for a in range(8):
                        P.tr(pT2[0:64, a * 128:(a + 1) * 128], qk_tm[:, a, :], ident[:], [t_qk, t_ident], [t_pT2])
                    qs = t % 2
                    P.cp('act', qTt[qs][:], pT2[0:64, 0:512], [t_pT2], [t_qTt[qs]])
                    P.cp('pool' if False else 'dve', kT[0:64, :, rows], pT2[0:64, 512:1024].rearrange("p (a t) -> p a t", a=4), [t_pT2], [t_kT])
                if 'cut2' not in SKIP and 'cut1' not in SKIP:
                    P.dma('sp', qTd[t], qTt[qs][:], [t_qTt[qs]], [], 'd_qT%d' % qs)
                if 'cut3' not in SKIP:
                    psC, tpsC = nextpA()
                    for k in range(8):
                        P.mm(psC[:, 0:140], hT[:, k, tsl], win_bf[:, k, 1024:1164], k == 0, k == 7, [t_hT, t_win], [tpsC])
                    P.cp('act', vsl[:, t, 0:64], psC[:, 0:64], [tpsC], [t_vsl])
                    P.cp('act', vwn[:, t, 0:64], psC[:, 64:128], [tpsC], [t_vwn])
                    P.act(gates[:, t, :], psC[:, 128:140], AF.Sigmoid, reads=[tpsC], writes=[t_gates])
            if 'gmlp' not in SKIP:
                psG, tpsG = nextpA()
                for k in range(8):
                    P.mm(psG[:, 0:384], hT[:, k, tsl], win_bf[:, k, 1164:1548], k == 0, k == 7, [t_hT, t_win], [tpsG])
                P.cp('act', gx[:], psG[:, 0:384], [tpsG], [t_gx])
                gelu_tanh(P, gg[:], gx[:], gtmp[:], 'pool', t_gx, t_gg, t_gtmp)
                P.op('dve', lambda e: e.bn_stats(gstats[:], gg[:, 128:384]), [t_gg], [t_gsm])
                P.op('dve', lambda e: e.bn_aggr(gmv[:], gstats[:]), [t_gsm], [t_gsm])
                P.ts('dve', gsc[:, 0:1], gmv[:, 1:2], 1e-5, None, ALU.add, reads=[t_gsm], writes=[t_gsm])
                P.act(gsc[:, 0:1], gsc[:, 0:1], AF.Sqrt, reads=[t_gsm], writes=[t_gsm])
                P.op('dve', lambda e: e.reciprocal(gsc[:, 1:2], gsc[:, 0:1]), [t_gsm], [t_gsm])
                P.stt(gsc[:, 2:3], gmv[:, 0:1], -1.0, gsc[:, 1:2], ALU.mult, ALU.mult, reads=[t_gsm], writes=[t_gsm])
                P.act(gvh[:], gg[:, 128:384], AF.Identity, scale=gsc[:, 1:2], bias=gsc[:, 2:3], reads=[t_gg, t_gsm], writes=[t_gvh])
                P.tt('dve', gvh[:, 0:128], gvh[:, 0:128], gnwb[:, 0:128], ALU.mult, [t_gvh, t_c1], [t_gvh])
                P.tt('dve', gvn[:], gvh[:, 0:128], gnbb[:, 0:128], ALU.add, [t_gvh, t_c1], [t_gvn])
                psS, tpsS = nextpA()
                for g in range(2):
                    P.mm(psS[:, g * 64:(g + 1) * 64], ws_bf[:, g, :], gvn[:, g * 64:(g + 1) * 64], True, True, [t_c1, t_gvn], [tpsS])
                for g in range(2):
                    P.stt(ogb[:, g * 64:(g + 1) * 64], psS[:, g * 64:(g + 1) * 64], bs[:, g:g + 1], gg[:, g * 64:(g + 1) * 64],
                          ALU.add, ALU.mult, reads=[tpsS, t_c1, t_gg], writes=[t_ogb])
                P.dma('sp', o_g[rows, :], ogb[:], [t_ogb], [], 'd_og')


    if 1 in stages:
        part_hT(0)
        for blk in range(NB):
            streams = [P.record(lambda: part_hgrn(blk)), P.record(lambda: part_tok(blk))]
            if blk + 1 < NB:
                streams.append(P.record(lambda: part_hT(blk + 1)))
            P.replay(*streams, gran=GRAN)
    P.barrier()
    es1.close()
    build_P23(nc, P, S, stages, locals())


def build_P23(nc, P, S, stages, env):
    v = NS(env)
    T = P.tok
    NT = S // 128
    nC = S // 16 - 1
    ident, identf, kT, vsl, vwn, gates = v.ident, v.identf, v.kT, v.vsl, v.vwn, v.gates
    t_ident, t_identf, t_kT, t_vsl, t_vwn, t_gates = v.t_ident, v.t_identf, v.t_kT, v.t_vsl, v.t_vwn, v.t_gates
    qTd = v.qTd
    sb = P.sb
    ps = P.ps
    kcT = sb("kcT", [64, 512], BF16)
    R = sb("R", [128, 4, 65], BF16)
    OV = sb("OV", [128, 4, 128], BF16)
    t_kcT, t_R, t_OV = T('kcT'), T('R'), T('OV')
    P.dma('sp', OV[:], v.c_ov, [], [t_OV], 'd_c5')
    P.op('pool', lambda e: e.memset(kcT[:], 0.0), [], [t_kcT])
    P.op('pool', lambda e: e.memset(R[:, :, 0:64], 0.0), [], [t_R])
    P.op('pool', lambda e: e.memset(R[:, :, 64:65], 1.0), [], [t_R])
    es2 = ExitStack()
    sb2 = lambda n, sh, dt: es2.enter_context(nc.sbuf_tensor(P.pfx + "a2_" + n, list(sh), dt))
    ps2 = lambda n, sh, dt: es2.enter_context(nc.psum_tensor(P.pfx + "q2_" + n, list(sh), dt))
    w1st = sb2("w1st", [64, 8, 256], F32)
    w1b = [sb2("w1b%d" % i, [64, 32, 256], BF16) for i in range(2)]
    w2st = sb2("w2st", [128, 2, 64], F32)
    w2b = [sb2("w2b%d" % i, [128, 2, 64], BF16) for i in range(2)]
    peT = sb2("peT", [64, 2, 32], F32)
    peTb = sb2("peTb", [64, 2, 32], BF16)
    pbias = sb2("pbias", [128, 2, 2], F32)
    hx = sb2("hx", [128, 512], F32)
    htmp = sb2("htmp", [128, 512], F32)
    hTb = sb2("hTb", [128, 2, 512], BF16)
    t_w1st, t_w2st, t_pe, t_pbias, t_hx, t_htmp, t_hTb = [T(n) for n in ('w1st', 'w2st', 'pe', 'pbias', 'hx', 'htmp', 'hTb')]
    t_w1b = [T('w1b0'), T('w1b1')]
    t_w2b = [T('w2b0'), T('w2b1')]
    pc = [ps2("pc%d" % i, [128, 512], F32) for i in range(2)]
    t_pc = [T('pc0'), T('pc1')]
    if 2 in stages:
        P.dma('sp', peT[:, 0, :], v.pek, [], [t_pe], 'd_c6')
        P.dma('sp', peT[:, 1, :], v.pev, [], [t_pe], 'd_c6')
        P.cp('dve', peTb[:], peT[:], [t_pe], [t_pe])
        for kv, (w1d, w2d) in enumerate(((v.w1k, v.w2k), (v.w1v, v.w2v))):
            w1v_ = w1d.rearrange("(j d) h -> d j h", d=64)
            for jq in range(4):
                P.dma('sp', w1st[:], w1v_[:, jq * 8:(jq + 1) * 8, :], [], [t_w1st], 'd_w1st')
                P.cp(['dve', 'pool'][jq % 2], w1b[kv][:, jq * 8:(jq + 1) * 8, :], w1st[:], [t_w1st], [t_w1b[kv]])
            P.dma('sp', w2st[:], w2d.rearrange("(c p) d -> p c d", p=128), [], [t_w2st], 'd_w2st')
            P.cp('dve', w2b[kv][:], w2st[:], [t_w2st], [t_w2b[kv]])
            for c2 in range(2):
                for j in range(32):
                    P.mm(pc[0][:, 0:1], w1b[kv][:, j, c2 * 128:(c2 + 1) * 128], peTb[:, kv, j:j + 1], j == 0, j == 31,
                         [t_w1b[kv], t_pe], [t_pc[0]])
                P.cp('dve', pbias[:, kv, c2:c2 + 1], pc[0][:, 0:1], [t_pc[0]], [t_pbias])
            src = kT[0:64, 0, :] if kv == 0 else kT[0:64, 3, :]
            P.op('pool', lambda e: e.memset(hTb[:], 0.0), [], [t_hTb])
            for c2 in range(2):
                pp, tpp = pc[c2 % 2], t_pc[c2 % 2]
                for j in range(32):
                    P.mm(pp[:, 0:nC], w1b[kv][:, j, c2 * 128:(c2 + 1) * 128], src[:, j:j + 16 * (nC - 1) + 1:16], j == 0, j == 31,
                         [t_w1b[kv], t_kT], [tpp])
                P.act(hx[:, 0:nC], pp[:, 0:nC], AF.Identity, bias=pbias[:, kv, c2:c2 + 1], reads=[tpp, t_pbias], writes=[t_hx])
                gelu_tanh(P, hTb[:, c2, 0:nC], hx[:, 0:nC], htmp[:, 0:nC], 'dve', t_hx, t_hTb, t_htmp)
            if kv == 0:
                for c2 in range(2):
                    P.mm(pc[0][0:64, 0:nC], w2b[kv][:, c2, :], hTb[:, c2, 0:nC], c2 == 0, c2 == 1, [t_w2b[kv], t_hTb], [t_pc[0]])
                P.cp('act', kcT[:, 0:nC], pc[0][0:64, 0:nC], [t_pc[0]], [t_kcT])
            else:
                for ntile in range((nC + 127) // 128):
                    for c2 in range(2):
                        P.mm(pc[1][:, 0:64], hTb[:, c2, ntile * 128:(ntile + 1) * 128], w2b[kv][:, c2, :], c2 == 0, c2 == 1,
                             [t_hTb, t_w2b[kv]], [t_pc[1]])
                    P.cp('act', R[:, ntile, 0:64], pc[1][:, 0:64], [t_pc[1]], [t_R])
    P.barrier()
    es2.close()
    if 3 not in stages:
        return
    causal4 = sb("causal4", [128, 4, 128], BF16)
    upper4 = sb("upper4", [128, 4, 128], BF16)
    cbv = sb("cbv", [128, 17, 128], BF16)
    fbig = sb("fbig", [128, 256], F32)
    t_c3 = T('c3')
    P.dma('sp', causal4[:], v.c_causal4, [], [t_c3], 'd_c7')
    P.dma('sp', upper4[:], v.c_upper4, [], [t_c3], 'd_c7')
    P.dma('sp', cbv[:], v.c_cbv, [], [t_c3], 'd_c7')
    P.dma('sp', fbig[:], v.c_fbig, [], [t_c3], 'd_c7')
    qt = [sb("qt%d" % i, [128, 512], BF16) for i in range(2)]
    qtB = [sb("qtB%d" % i, [128, 512], BF16) for i in range(2)]
    t_qt = [T('qt0'), T('qt1')]
    t_qtB = [T('qtB0'), T('qtB1')]
    negb3 = sb("negb3", [128, 192], BF16)
    P.op('pool', lambda e: e.memset(negb3[:], 0.0), [], [T('negb3i')])
    for i in range(2):
        P.op('pool', lambda e, i=i: e.memset(qt[i][:], 0.0), [], [t_qt[i]])
        P.op('pool', lambda e, i=i: e.memset(qtB[i][:], 0.0), [], [t_qtB[i]])
    cb4 = sb("cb4", [128, 4, 128], BF16)
    cb4p = sb("cb4p", [128, 4, 128], BF16)
    t_cb4, t_cb4p = T('cb4'), T('cb4p')
    P.cp('pool', cb4p[:], cbv[:, 16, :].unsqueeze(1).to_broadcast([128, 4, 128]), [t_c3], [t_cb4p])
    osbT = sb("osbT", [65, 512], F32)
    osbT2 = sb("osbT2", [65, 512], F32)
    t_osbT2 = T('osbT2')
    ty1 = sb("ty1", [128, 128], F32)
    ty2 = sb("ty2", [128, 128], F32)
    t_ty1, t_ty2 = T('ty1'), T('ty2')
    impsb = sb("impsb", [128, 512], F32)
    t_osbT, t_impsb = T('osbT'), T('impsb')
    zz = sb("zz", [128, 3, 4], F32)
    t_zz = [T('zz0'), T('zz1'), T('zz2')]
    accs = [sb("acc%d" % i, [128, 4, 64], F32) for i in range(2)]
    t_accs = [T('acc0'), T('acc1')]
    accb = sb("accb", [128, 256], BF16)
    t_accb = T('accb')
    imp = sb("imp", [128, 128], F32)
    sc2 = sb("sc2", [128, 128], F32)
    m8 = sb("m8", [128, 16], F32)
    negb = sb("negb", [128, 128], BF16)
    nsT4 = sb("nsT4", [128, 4, 128], BF16)
    t_imp, t_sc2, t_m8, t_negb, t_nsT4 = [T(n) for n in ('imp', 'sc2', 'm8', 'negb', 'nsT4')]
    s_ps = [ps("s_ps%d" % i, [128, 512], F32) for i in range(2)]
    t_sps = [T('p_sps0'), T('p_sps1')]
    oT_ps = [ps("oT_ps%d" % i, [128, 512], F32) for i in range(3)]
    t_oT = [T('p_oT0'), T('p_oT1'), T('p_oT2')]
    impT_ps = ps("impT_ps", [128, 512], F32)
    t_impT = T('p_impT')
    tpA = ps("tpA", [128, 512], F32)
    t_tpA = T('p_tpA')
    tpB = impT_ps[:].bitcast(BF16)
    t_tpB = t_impT
    tpA2 = ps("tpA2", [128, 512], F32)
    t_tpA2 = T('p_tpA2')
    si = [0]

    class Stream:
        def __init__(self, name, npb):
            self.pend = []
            self.pb = [sb("pb%s%d" % (name, i), [128, 512], BF16) for i in range(npb)]
            self.t_pb = [T('pb%s%d' % (name, i)) for i in range(npb)]
            self.pi = 0

        def flush(self):
            while self.pend:
                self.pend.pop(0)()

    SX = Stream('X', 2)
    SY = Stream('Y', 3)

    def score_tile(st, lhsT, lhs_toks, masks, qtile, tq):
        i = si[0] % 2
        si[0] += 1
        sp_, tsp = s_ps[i], t_sps[i]
        j = st.pi % len(st.pb)
        st.pi += 1

        def grp():
            P.mm(sp_[:], lhsT, qtile, True, len(masks) == 0, lhs_toks + [tq], [tsp])
            for mi, (ml, mr, mt) in enumerate(masks):
                P.mm(sp_[:], ml, mr, False, mi == len(masks) - 1, mt, [tsp])
            P.act(st.pb[j][:], sp_[:], AF.Exp, scale=0.125, reads=[tsp], writes=[st.t_pb[j]])
        P.atomic(grp)
        st.flush()
        return st.pb[j], st.t_pb[j]

    def finish_branch(br, tp_, t_tp, osb_, t_osb):
        def grp():
            P.cp('act', osb_[:], oT_ps[br][0:65, :], [t_oT[br]], [t_osb])
            for g in range(4):
                P.tr(tp_[:, g * 65:(g + 1) * 65], osb_[:, g * 128:(g + 1) * 128], identf[0:65, 0:65], [t_osb, t_identf], [t_tp])
        P.atomic(grp)
        tv = tp_[:, 0:260].rearrange("p (g c) -> p g c", g=4)
        P.ts('dve', zz[:, br, :], tv[:, :, 64], 1e-30, None, ALU.max, reads=[t_tp], writes=[t_zz[br]])
        P.op('dve', lambda e: e.reciprocal(zz[:, br, :], zz[:, br, :]), [t_zz[br]], [t_zz[br]])
        return tv

    def partX(qb):
        qs = qb % 2
        P.dma('sp', qt[qs][0:64, :], qTd[qb], [], [t_qt[qs]], 'd_qt%d' % qs)
        if qb >= 32:
            P.dma('sp', qtB[qs][0:64, :], qTd[qb], [], [t_qtB[qs]], 'd_qtB%d' % qs)
        a = qb % 16
        ntl = qb // 16
        P.cp('pool', cb4[:], cbv[:, a, :].unsqueeze(1).to_broadcast([128, 4, 128]), [t_c3], [t_cb4])
        for nt in range(ntl + 1):
            masks = []
            if nt == ntl:
                masks.append((ident[:], cb4[:].rearrange("p g q -> p (g q)"), [t_ident, t_cb4]))
            elif nt == ntl - 1 and a == 0:
                masks.append((ident[:], cb4p[:].rearrange("p g q -> p (g q)"), [t_ident, t_cb4p]))
            pt, tpt = score_tile(SX, kcT[:, nt * 128:(nt + 1) * 128], [t_kcT], masks, qt[qs][0:64, :], t_qt[qs])
            def pv(nt=nt, pt=pt, tpt=tpt):
                P.mm(oT_ps[0][0:65, :], R[:, nt, :], pt[:], nt == 0, nt == ntl, [t_R, tpt], [t_oT[0]])
                P.mm(impT_ps[:], OV[:, nt, :], pt[:], nt == 0, nt == ntl, [t_OV, tpt], [t_impT])
            SX.pend.append(pv)
        SX.flush()
        tv = finish_branch(0, tpA, t_tpA, osbT, t_osbT)
        gv_ = gates[:, qb, :].rearrange("p (g c) -> p g c", c=3)
        P.tt('dve', zz[:, 0, :], zz[:, 0, :], gv_[:, :, 0], ALU.mult, [t_zz[0], t_gates], [t_zz[0]])
        P.cp('act', impsb[:], impT_ps[:], [t_impT], [t_impsb])
        P.tt('dve', accs[qs][:], tv[:, :, 0:64], zz[:, 0, :].unsqueeze(2).to_broadcast([128, 4, 64]), ALU.mult,
             [t_tpA, t_zz[0]], [t_accs[qs]])
        P.ts('dve', m8[:, 8:12], tv[:, :, 64], 1e-30, None, ALU.max, reads=[t_tpA], writes=[t_m8])
        P.op('dve', lambda e: e.reciprocal(m8[:, 8:12], m8[:, 8:12]), [t_m8], [t_m8])
        for g in range(4):
            P.tr(tpA[:, g * 128:(g + 1) * 128], impsb[:, g * 128:(g + 1) * 128], identf[:], [t_impsb, t_identf, t_accs[qs], t_m8], [t_tpA])
        P.ts('dve', imp[:], tpA[:, 0:128], m8[:, 8:9], None, ALU.mult, reads=[t_tpA, t_m8], writes=[t_imp])
        for g in range(1, 4):
            P.stt(imp[:], tpA[:, g * 128:(g + 1) * 128], m8[:, 8 + g:9 + g], imp[:], ALU.mult, ALU.add,
                  reads=[t_tpA, t_m8, t_imp], writes=[t_imp])
        P.tt('dve', imp[:], imp[:], fbig[:, 127 - 2 * qb:255 - 2 * qb], ALU.add, [t_imp, t_c3], [t_imp])
        P.ts('dve', imp[:, 0:1], imp[:, 0:1], 100.0, None, ALU.add, reads=[t_imp], writes=[t_imp])
        P.op('dve', lambda e: e.max(m8[:, 0:8], imp[:]), [t_imp], [t_m8])
        P.op('dve', lambda e: e.match_replace(sc2[:], m8[:, 0:8], imp[:], -1e9), [t_imp, t_m8], [t_sc2])
        P.op('dve', lambda e: e.max(m8[:, 0:8], sc2[:]), [t_sc2], [t_m8])
        P.ts('dve', sc2[:], imp[:], m8[:, 7:8], None, ALU.is_ge, reads=[t_imp, t_m8], writes=[t_sc2])
        P.ts('dve', negb3[:, 64:192], sc2[:], -1.0, -NEGB, ALU.add, ALU.mult, reads=[t_sc2], writes=[t_negb])
        P.tr(tpB[:, 0:128], negb3[:, 0:128], ident[:], [t_negb, t_ident], [t_tpB])
        if qb >= 32:
            P.tr(tpB[:, 128:256], negb3[:, 64:192], ident[:], [t_negb, t_ident], [t_tpB])
        P.cp('dve', qt[qs][64:128, :].rearrange("p (g q) -> p g q", g=4),
             tpB[64:128, 0:128].unsqueeze(1).to_broadcast([64, 4, 128]), [t_tpB], [t_qt[qs]])
        if qb >= 32:
            P.cp('dve', qtB[qs][64:128, :].rearrange("p (g q) -> p g q", g=4),
                 tpB[64:128, 128:256].unsqueeze(1).to_broadcast([64, 4, 128]), [t_tpB], [t_qtB[qs]])

    def partY(qb):
        qs = qb % 2
        for kt in range(qb + 1):
            masks = []
            if kt == qb:
                masks.append((ident[:], causal4[:].rearrange("p g q -> p (g q)"), [t_ident, t_c3]))
            qq, tqq = (qt[qs], t_qt[qs]) if kt < 32 else (qtB[qs], t_qtB[qs])
            pt, tpt = score_tile(SY, kT[:, 1, kt * 128:(kt + 1) * 128], [t_kT], masks, qq[:], tqq)
            SY.pend.append(lambda kt=kt, pt=pt, tpt=tpt: P.mm(oT_ps[1][0:65, :], vsl[:, kt, :], pt[:], kt == 0, kt == qb,
                                                           [t_vsl, tpt], [t_oT[1]]))
        k0 = max(0, qb - 4)
        for kt in range(k0, qb + 1):
            masks = []
            if kt == qb:
                masks.append((ident[:], causal4[:].rearrange("p g q -> p (g q)"), [t_ident, t_c3]))
            elif kt == qb - 4:
                masks.append((ident[:], upper4[:].rearrange("p g q -> p (g q)"), [t_ident, t_c3]))
            pt, tpt = score_tile(SY, kT[:, 2, kt * 128:(kt + 1) * 128], [t_kT], masks, qt[qs][:], t_qt[qs])
            SY.pend.append(lambda kt=kt, pt=pt, tpt=tpt: P.mm(oT_ps[2][0:65, :], vwn[:, kt, :], pt[:], kt == k0, kt == qb,
                                                           [t_vwn, tpt], [t_oT[2]]))
        SY.flush()
        for br in (1, 2):
            tv = finish_branch(br, tpA2, t_tpA2, osbT2, t_osbT2)
            P.tt('dve', zz[:, br, :], zz[:, br, :], gates[:, qb, :].rearrange("p (g c) -> p g c", c=3)[:, :, br], ALU.mult, [t_zz[br], t_gates], [t_zz[br]])
            P.tt('dve', ty1[:].rearrange("p (g c) -> p g c", g=4)[:, :, 0:32], tv[:, :, 0:32],
                 zz[:, br, :].unsqueeze(2).to_broadcast([128, 4, 32]), ALU.mult, [t_tpA2, t_zz[br]], [t_ty1])
            P.tt('dve', ty2[:].rearrange("p (g c) -> p g c", g=4), tv[:, :, 32:64],
                 zz[:, br, :].unsqueeze(2).to_broadcast([128, 4, 32]), ALU.mult, [t_tpA2, t_zz[br]], [t_ty2])
            P.tt('dve', accs[qs][:, :, 0:32], accs[qs][:, :, 0:32], ty1[:].rearrange("p (g c) -> p g c", g=4), ALU.add, [t_accs[qs], t_ty1], [t_accs[qs]])
            P.tt('dve', accs[qs][:, :, 32:64], accs[qs][:, :, 32:64], ty2[:].rearrange("p (g c) -> p g c", g=4), ALU.add, [t_accs[qs], t_ty2], [t_accs[qs]])
        P.cp('act', accb[:], accs[qs][:].rearrange("p g c -> p (g c)"), [t_accs[qs]], [t_accb])
        P.dma('sp', v.o_n[qb * 128:(qb + 1) * 128, :], accb[:], [t_accb], [], 'd_on')

    partX(0)
    for qb in range(NT):
        streams = [P.record(lambda: partY(qb))]
        if qb + 1 < NT:
            streams.append(P.record(lambda: partX(qb + 1)))
        P.replay(*streams)


def p_inputs(l, b, hh, x_full, inp, consts):
    S = x_full.shape[1]
    NT = S // 128
    win = inp['w_in'][l]
    h0 = hh * 128
    cols = []
    for q in range(4):
        cols.append(np.arange(q * 256 + h0, q * 256 + h0 + 128))
    cols.append(np.arange(1024 + hh * 256, 1024 + hh * 256 + 256))
    for base in (1536, 1792, 2048, 1664):
        cols.append(np.arange(base + hh * 64, base + hh * 64 + 64))
    for base in (1920, 2176):
        cols.append(np.arange(base + hh * 64, base + hh * 64 + 64))
    cols.append(np.arange(2304 + hh * 12, 2304 + hh * 12 + 12))
    cols.append(np.arange(2328 + hh * 128, 2328 + hh * 128 + 128))
    gperm = np.concatenate([np.arange(hh * 128, hh * 128 + 128), np.arange((1 - hh) * 128, (1 - hh) * 128 + 128)])
    cols.append(2584 + gperm)
    cols = np.concatenate(cols)
    assert cols.shape[0] == NCOL
    lb = inp['hgrn_lower_bounds']
    lbT = np.ascontiguousarray(lb[:, h0:h0 + 128].reshape(4, 2, 64).transpose(2, 1, 0))
    lsel = np.zeros((64, 2, 4), np.float32)
    lsel[:, :, 1:l + 1] = 1.0
    d = {
        "x": np.ascontiguousarray(x_full[b]),
        "cT": np.ascontiguousarray(inp['c'][b].reshape(8, 128).T),
        "wada": np.ascontiguousarray(inp['w_ada'][l][:, 0:2048]),
        "badaT": np.ascontiguousarray(inp['b_ada'][l][0:2048].reshape(16, 128).T),
        "win": np.ascontiguousarray(win[:, cols]),
        "posT": np.ascontiguousarray(inp['positions'][b].reshape(NT, 128).T.astype(np.int32)),
        "lbT": lbT.astype(np.float32), "lsel": lsel,
        "hnw": np.ascontiguousarray(inp['hgrn_norm_w'][l][None, :]),
        "pekT": np.ascontiguousarray(inp['cmp_pe_k'][l].T), "pevT": np.ascontiguousarray(inp['cmp_pe_v'][l].T),
        "w1k": inp['cmp_w1_k'][l], "w1v": inp['cmp_w1_v'][l], "w2k": inp['cmp_w2_k'][l], "w2v": inp['cmp_w2_v'][l],
        "gnw": np.ascontiguousarray(inp['gmlp_norm_w'][l][gperm][None, :]),
        "gnb": np.ascontiguousarray(inp['gmlp_norm_b'][l][gperm][None, :]),
        "wsT": np.ascontiguousarray(inp['gmlp_w_s'][l][2 * hh:2 * hh + 2].transpose(0, 2, 1)),
        "bsT": np.ascontiguousarray(inp['gmlp_b_s'][l][2 * hh:2 * hh + 2].T),
    }
    for k in C_KEYS:
        d[k] = consts[k]
    return d


def assemble_ocat(o0, o1):
    return np.concatenate([o0[:, 0:128], o1[:, 0:128], o0[:, 128:384], o1[:, 128:384], o0[:, 384:512], o1[:, 384:512]], axis=1)


_CACHE = {}

P_KEYS_LH = ('win', 'gnw', 'gnb', 'wsT', 'bsT')
P_KEYS_L = ('wada', 'badaT', 'lsel', 'hnw', 'pekT', 'pevT', 'w1k', 'w1v', 'w2k', 'w2v')
P_KEYS_H = ('lbT',)
F_KEYS_L = ('wada', 'bada', 'wo', 'w1', 'w2', 'lnp')
C_KEYS = ('ident', 'identf', 'tri64', 'resetm', 'tri128', 'causal4', 'upper4', 'cbv', 'ehalf', 'ov', 'fbig', 'invf')
P_NAME = {'pekT': 'pek', 'pevT': 'pev'}


def build_fused(S):
    D = D_MODEL
    NT = S // 128
    L = DEPTH
    nc = bass.Bass("TRN2", target_bir_lowering=False)

    def din(name, shape, dt=F32):
        return nc.dram_tensor(name, list(shape), dt, kind="ExternalInput").ap()

    shp = {'win': [D, NCOL], 'gnw': [1, 256], 'gnb': [1, 256], 'wsT': [2, 128, 128], 'bsT': [128, 2],
           'wada': [D, 2048], 'badaT': [128, 16], 'lsel': [64, 2, 4], 'hnw': [1, 64], 'pekT': [64, 32], 'pevT': [64, 32],
           'w1k': [2048, 256], 'w1v': [2048, 256], 'w2k': [256, 64], 'w2v': [256, 64], 'lbT': [64, 2, 4]}
    fshp = {'wada': [D, 4096], 'bada': [1, 4096], 'wo': [D, D], 'w1': [D, D_FF], 'w2': [D_FF, D], 'lnp': [4, D]}
    cshp = {'ident': ([128, 128], BF16), 'identf': ([128, 128], F32), 'tri64': ([64, 2, 64], F32), 'resetm': ([64, 512], F32),
            'tri128': ([128, 128], F32), 'causal4': ([128, 4, 128], BF16), 'upper4': ([128, 4, 128], BF16),
            'cbv': ([128, 17, 128], BF16), 'ehalf': ([64, S], BF16), 'ov': ([128, 4, 128], BF16), 'fbig': ([128, 256], F32),
            'invf': ([128, 32], F32)}
    x_in = din("x", [S, D])
    cT = din("cT", [128, 8])
    posT = din("posT", [128, NT], I32)
    dp = {}
    for k in P_KEYS_LH:
        dp[k] = din("p_" + k, [L, 2] + shp[k])
    for k in P_KEYS_L:
        dp[k] = din("p_" + k, [L] + shp[k])
    for k in P_KEYS_H:
        dp[k] = din("p_" + k, [2] + shp[k])
    df = {k: din("f_" + k, [L] + fshp[k]) for k in F_KEYS_L}
    dc = {k: din(k, cshp[k][0], cshp[k][1]) for k in C_KEYS}
    xbuf = [nc.dram_tensor("xbuf%d" % i, [S, D], F32).ap() for i in range(2)]
    ocat_d = nc.dram_tensor("ocat_d", [S, D], BF16).ap()
    qTd = nc.dram_tensor("qTd", [NT, 64, 512], BF16).ap()
    xo = nc.dram_tensor("xo", [S, D], F32, kind="ExternalOutput").ap()

    P = Prog(nc)
    for l in range(L):
        xsrc = x_in if l == 0 else xbuf[(l - 1) % 2]
        xdst = xo if l == L - 1 else xbuf[l % 2]
        for hh in range(2):
            A = NS({})
            A.x, A.cT, A.posT = xsrc, cT, posT
            for k in P_KEYS_LH:
                setattr(A, P_NAME.get(k, k), dp[k][l, hh])
            for k in P_KEYS_L:
                setattr(A, P_NAME.get(k, k), dp[k][l])
            for k in P_KEYS_H:
                setattr(A, P_NAME.get(k, k), dp[k][hh])
            for k in C_KEYS:
                setattr(A, 'c_' + k, dc[k])
            A.o_h = ocat_d[:, hh * 128:(hh + 1) * 128]
            A.o_n = ocat_d[:, 256 + hh * 256:256 + (hh + 1) * 256]
            A.o_g = ocat_d[:, 768 + hh * 128:768 + (hh + 1) * 128]
            A.qTd = qTd
            P.begin_phase("L%dh%d_" % (l, hh))
            emit_P(nc, P, S, A)
            P.end_phase()
        A = NS({})
        A.x, A.oc, A.cT, A.xo = xsrc, ocat_d, cT, xdst
        A.wada, A.bada, A.wo, A.w1, A.w2, A.lnp = [df[k][l] for k in F_KEYS_L]
        A.identd, A.identfd = dc['ident'], dc['identf']
        P.begin_phase("L%df_" % l)
        emit_F(nc, P, S, A)
        P.end_phase()
    P.emit()
    return nc


def fused_inputs(b, inp, consts):
    S = inp['x'].shape[1]
    per = [[p_inputs(l, b, hh, inp['x'], inp, consts) for hh in range(2)] for l in range(DEPTH)]
    d = {"x": per[0][0]['x'], "cT": per[0][0]['cT'], "posT": per[0][0]['posT']}
    for k in P_KEYS_LH:
        d["p_" + k] = np.stack([np.stack([per[l][hh][k] for hh in range(2)]) for l in range(DEPTH)])
    for k in P_KEYS_L:
        d["p_" + k] = np.stack([per[l][0][k] for l in range(DEPTH)])
    for k in P_KEYS_H:
        d["p_" + k] = np.stack([per[0][hh][k] for hh in range(2)])
    for l in range(DEPTH):
        pass
    fl = [f_inputs(l, b, slice(0, 1), inp['x'], np.zeros((1, D_MODEL), NPBF), inp, consts) for l in range(DEPTH)]
    for k in F_KEYS_L:
        d["f_" + k] = np.stack([fl[l][k] for l in range(DEPTH)])
    for k in C_KEYS:
        d[k] = consts[k]
    return d


def kernel(**inputs):
    inp = {k: np.asarray(v) for k, v in inputs.items()}
    inp['x'] = np.ascontiguousarray(inp['x'], dtype=np.float32)
    B, S, D = inp['x'].shape
    n = 8
    if 'fused' not in _CACHE:
        _CACHE['fused'] = build_fused(S)
        _CACHE['c'] = p_consts(S)
    nc, consts = _CACHE['fused'], _CACHE['c']
    maps = [fused_inputs(b, inp, consts) for b in range(B)]
    in_maps = [maps[c % B] for c in range(n)]
    res = run_bass_kernel_spmd(nc, in_maps, core_ids=list(range(n)))
    return np.stack([res.results[b]['xo'] for b in range(B)]).astype(np.float32)
```

```python
import numpy as np
from contextlib import ExitStack
import ml_dtypes
import concourse.bass as bass
import concourse.mybir as mybir
from concourse.bass_utils import run_bass_kernel_spmd

F32 = mybir.dt.float32
BF16 = mybir.dt.bfloat16
I32 = mybir.dt.int32
AF = mybir.ActivationFunctionType
ALU = mybir.AluOpType
AX = mybir.AxisListType
NPBF = ml_dtypes.bfloat16

D_MODEL = 1024
D_FF = 4096
DEPTH = 4
DN_ALPHA = (2 * DEPTH) ** 0.25
ENG = ['pe', 'act', 'dve', 'pool', 'sp']


class NS:
    def __init__(self, d):
        self.__dict__.update(d)


class Tok:
    __slots__ = ('w', 'r', 'name', 'ps')

    def __init__(self, name='', ps=False):
        self.w = None
        self.r = {}
        self.name = name
        self.ps = ps


class Prog:
    def __init__(self, nc):
        self.nc = nc
        self.es = ExitStack()
        self.ses = ExitStack()
        self.pfx = ''
        self.q = {e: [] for e in ENG}
        self.cnt = {}
        self.seen = {e: {} for e in ENG}
        self.semh = {}
        self.isdma = set()
        self.ntok = 0
        self._rec = None

    def tok(self, name='', ps=None):
        if ps is None:
            ps = name.startswith('p')
        return Tok(name, ps)

    def toks(self, n, name=''):
        return [Tok(name + str(i)) for i in range(n)]

    def sem(self, key):
        if key not in self.semh:
            self.semh[key] = self.ses.enter_context(self.nc.semaphore(key))
            self.cnt[key] = 0
        return self.semh[key]

    def sb(self, name, shape, dt):
        return self.es.enter_context(self.nc.sbuf_tensor(self.pfx + "sb_" + name, list(shape), dt))

    def ps(self, name, shape, dt):
        return self.es.enter_context(self.nc.psum_tensor(self.pfx + "ps_" + name, list(shape), dt))

    def _deps(self, eng, reads, writes):
        waits = {}

        def need(ev):
            if ev is None:
                return
            k, v = ev
            if eng == 'pe' and k == 's_pe':
                return
            if k in self.isdma:
                v = self.cnt[k]
            if self.seen[eng].get(k, 0) >= v:
                return
            if waits.get(k, 0) < v:
                waits[k] = v

        own = 's_' + eng
        for t in reads:
            need(t.w)
            if t.ps:
                for k, v in t.r.items():
                    if k != own:
                        need((k, v))
        for t in writes:
            need(t.w)
            for k, v in t.r.items():
                need((k, v))
        for k, v in waits.items():
            self.seen[eng][k] = v
        return list(waits.items())

    def _commit(self, ev, reads, writes):
        k, v = ev
        for t in reads:
            if t.r.get(k, 0) < v:
                t.r[k] = v
        for t in writes:
            t.w = ev
            t.r = {}

    def record(self, fn):
        saved, self._rec = self._rec, []
        fn()
        out, self._rec = self._rec, saved
        return out

    def atomic(self, fn):
        if self._rec is None:
            fn()
            return
        grp = self.record(fn)
        flat = []
        for kd, ar in grp:
            if kd == 'grp':
                flat.extend(ar)
            else:
                flat.append((kd, ar))
        self._rec.append(('grp', flat))

    def replay(self, *streams, gran=1, speeds=None):
        if speeds is None:
            speeds = [1.0] * len(streams)
        speeds = [sp for st, sp in zip(streams, speeds) if st]
        streams = [st for st in streams if st]
        if gran > 1:
            streams = [[('grp', st[i:i + gran]) for i in range(0, len(st), gran)] for st in streams]
        idx = [0] * len(streams)
        while True:
            best, bf = -1, 2.0
            for i, st in enumerate(streams):
                if idx[i] < len(st):
                    f = idx[i] / len(st) / speeds[i]
                    if f < bf:
                        best, bf = i, f
            if best < 0:
                break
            kind, args = streams[best][idx[best]]
            idx[best] += 1
            for kd, ar in (args if kind == 'grp' else [(kind, args)]):
                if kd == 'op':
                    self.op(*ar)
                else:
                    self.dma(*ar)

    def op(self, eng, fn, reads=(), writes=()):
        if self._rec is not None:
            self._rec.append(('op', (eng, fn, reads, writes)))
            return
        waits = self._deps(eng, reads, writes)
        key = 's_' + eng
        self.sem(key)
        self.cnt[key] += 1
        ev = (key, self.cnt[key])
        self.q[eng].append((waits, fn, key, 1))
        self._commit(ev, reads, writes)

    def dma(self, eng, out, in_, reads=(), writes=(), key=None):
        if self._rec is not None:
            self._rec.append(('dma', (eng, out, in_, reads, writes, key)))
            return
        key = key + '_' + eng
        waits = self._deps(eng, reads, writes)
        self.sem(key)
        self.isdma.add(key)
        self.cnt[key] += 16
        ev = (key, self.cnt[key])
        self.q[eng].append((waits, (lambda e, o=out, i=in_: e.dma_start(out=o, in_=i)), key, 16))
        self._commit(ev, reads, writes)

    def coll(self, kind, op, groups, ins, outs, reads=(), writes=(), key=None):
        key = key + '_cc'
        waits = self._deps('pool', reads, writes)
        self.sem(key)
        self.isdma.add(key)
        self.cnt[key] += 16
        ev = (key, self.cnt[key])
        self.q['pool'].append((waits, (lambda e: e.collective_compute(kind, op, replica_groups=groups, ins=ins, outs=outs)),
                               key, 16))
        self._commit(ev, reads, writes)

    def mm(self, out, lhsT, rhs, start, stop, reads=(), writes=()):
        self.op('pe', lambda e: e.matmul(out, lhsT, rhs, start=start, stop=stop,
                                         skip_group_check=True), reads, writes)

    def tr(self, out, in_, ident, reads=(), writes=()):
        self.op('pe', lambda e: e.transpose(out, in_, ident), reads, writes)

    def act(self, out, in_, func, scale=1.0, bias=None, reads=(), writes=(), accum_out=None):
        def fn(e):
            kw = {}
            if bias is not None:
                kw['bias'] = bias
            if accum_out is not None:
                kw['accum_out'] = accum_out
            return e.activation(out, in_, func, scale=scale, **kw)
        self.op('act', fn, reads, writes)

    def tt(self, eng, out, a, b, op, reads=(), writes=()):
        self.op(eng, lambda e: e.tensor_tensor(out, a, b, op), reads, writes)

    def ts(self, eng, out, a, s1, s2, op0, op1=None, reads=(), writes=()):
        if op1 is None:
            self.op(eng, lambda e: e.tensor_scalar(out, a, s1, None, op0), reads, writes)
        else:
            self.op(eng, lambda e: e.tensor_scalar(out, a, s1, s2, op0, op1), reads, writes)

    def stt(self, out, a, s, b, op0, op1, reads=(), writes=()):
        self.op('dve', lambda e: e.scalar_tensor_tensor(out, a, s, b, op0, op1), reads, writes)

    def cp(self, eng, out, in_, reads=(), writes=()):
        if eng == 'act':
            self.op('act', lambda e: e.copy(out, in_), reads, writes)
        else:
            self.op(eng, lambda e: e.tensor_copy(out, in_), reads, writes)

    def barrier(self):
        snapshot = dict(self.cnt)
        for eng in ENG:
            waits = []
            for k, v in snapshot.items():
                if v == 0 or self.seen[eng].get(k, 0) >= v:
                    continue
                waits.append((k, v))
                self.seen[eng][k] = v
            key = 's_' + eng
            self.sem(key)
            self.cnt[key] += 1
            self.q[eng].append((waits, (lambda e: e.nop()), key, 1))

    def emit(self):
        nc = self.nc
        prog = self

        def mk(en):
            def body(e):
                for waits, fn, key, inc in prog.q[en]:
                    for k, v in waits:
                        e.wait_ge(prog.semh[k], v)
                    fn(e).then_inc(prog.semh[key], inc)
                if en == 'sp':
                    for k in sorted(prog.isdma):
                        e.wait_ge(prog.semh[k], prog.cnt[k])
            return body

        with nc.Block() as block:
            block.tensor(mk('pe'))
            block.scalar(mk('act'))
            block.vector(mk('dve'))
            block.gpsimd(mk('pool'))
            block.sync(mk('sp'))
        self.es.close()
        self.ses.close()

    def begin_phase(self, pfx):
        self.pfx = pfx
        self.es = ExitStack()

    def end_phase(self):
        self.barrier()
        self.es.close()
        self.es = ExitStack()


def ln_tile(P, src, dst_hat, stats, mv, sc, eps, t_src, t_dst, t_small):
    for j in range(2):
        P.op('dve', lambda e, j=j: e.bn_stats(stats[:, j, :], src[:, j * 512:(j + 1) * 512]),
             [t_src], [t_small] if j == 0 else [t_small])
    P.op('dve', lambda e: e.bn_aggr(mv[:, 0:2], stats[:].rearrange("p a b -> p (a b)")), [t_small], [t_small])
    P.ts('dve', sc[:, 0:1], mv[:, 1:2], eps, None, ALU.add, reads=[t_small], writes=[t_small])
    P.act(sc[:, 0:1], sc[:, 0:1], AF.Sqrt, reads=[t_small], writes=[t_small])
    P.op('dve', lambda e: e.reciprocal(sc[:, 1:2], sc[:, 0:1]), [t_small], [t_small])
    P.stt(sc[:, 2:3], mv[:, 0:1], -1.0, sc[:, 1:2], ALU.mult, ALU.mult, reads=[t_small], writes=[t_small])
    P.act(dst_hat, src, AF.Identity, scale=sc[:, 1:2], bias=sc[:, 2:3], reads=[t_src, t_small], writes=[t_dst])


def build_F(TOK):
    D, DFF = D_MODEL, D_FF
    nc = bass.Bass("TRN2", target_bir_lowering=False)
    A = NS({})
    A.x = nc.dram_tensor("x", [TOK, D], F32, kind="ExternalInput").ap()
    A.oc = nc.dram_tensor("ocat", [TOK, D], BF16, kind="ExternalInput").ap()
    A.cT = nc.dram_tensor("cT", [128, 8], F32, kind="ExternalInput").ap()
    A.wada = nc.dram_tensor("wada", [D, 4096], F32, kind="ExternalInput").ap()
    A.bada = nc.dram_tensor("bada", [1, 4096], F32, kind="ExternalInput").ap()
    A.wo = nc.dram_tensor("wo", [D, D], F32, kind="ExternalInput").ap()
    A.w1 = nc.dram_tensor("w1", [D, DFF], F32, kind="ExternalInput").ap()
    A.w2 = nc.dram_tensor("w2", [DFF, D], F32, kind="ExternalInput").ap()
    A.lnp = nc.dram_tensor("lnp", [4, D], F32, kind="ExternalInput").ap()
    A.identd = nc.dram_tensor("ident", [128, 128], BF16, kind="ExternalInput").ap()
    A.identfd = nc.dram_tensor("identf", [128, 128], F32, kind="ExternalInput").ap()
    A.xo = nc.dram_tensor("xo", [TOK, D], F32, kind="ExternalOutput").ap()
    P = Prog(nc)
    emit_F(nc, P, TOK, A)
    P.emit()
    return nc


def emit_F(nc, P, TOK, A):
    D, DFF = D_MODEL, D_FF
    NT = TOK // 128
    x, oc, cT, wada, bada, wo, w1, w2, lnp, identd, identfd, xo = (A.x, A.oc, A.cT, A.wada, A.bada, A.wo, A.w1, A.w2,
                                                                 A.lnp, A.identd, A.identfd, A.xo)
    ident = P.sb("ident_sb", [128, 128], BF16)
    identf = P.sb("identf_sb", [128, 128], F32)
    wo_bf = P.sb("wo_bf", [128, 8, D], BF16)
    w1_bf = P.sb("w1_bf", [128, 8, DFF], BF16)
    w2_bf = P.sb("w2_bf", [128, 32, D], BF16)
    NST = 2
    stage = [P.sb("stage%d" % i, [128, 1024], F32) for i in range(NST)]
    gb = P.sb("gb", [128, 2 * D], F32)
    lnb = P.sb("lnb", [128, 4, D], F32)
    scT = P.sb("scT", [128, 8], F32)
    ABt = P.sb("ABt", [128, 2, 8], F32)
    xt = P.sb("xt", [128, D], F32)
    oct_ = P.sb("oct", [128, D], BF16)
    oT = P.sb("oT", [128, 8, 128], BF16)
    r1 = P.sb("r1", [128, D], F32)
    xh = P.sb("xh", [128, D], F32)
    xhb = P.sb("xhb", [128, D], BF16)
    h2T = P.sb("h2T", [128, 8, 128], BF16)
    aT = P.sb("aT", [128, 32, 128], BF16)
    rep = aT[:, 0:16, :].rearrange("p a b -> p (a b)").bitcast(F32).rearrange("p (k j) -> p k j", k=8)
    rl = [P.sb("rl%d" % i, [128, 512], BF16) for i in range(2)]
    stats = P.sb("stats", [128, 2, 6], F32)
    mv = P.sb("mv", [128, 2], F32)
    sc = P.sb("sc", [128, 4], F32)
    ps_T = P.ps("ps_T", [128, 8, 128], BF16)
    ps_big = P.ps("ps_big", [128, D], F32)
    ps_a = [P.ps("ps_a%d" % i, [128, 512], F32) for i in range(2)]

    T = lambda n: P.tok(n)
    t_ident, t_identf = T('ident'), T('identf')
    t_wo, t_w1, t_w2 = T('wo'), T('w1'), T('w2')
    t_stage = [T('st%d' % i) for i in range(NST)]
    t_gb, t_lnb, t_scT, t_ABt = T('gb'), T('lnb'), T('scT'), T('ABt')
    t_xt, t_oct, t_oT, t_r1, t_xh, t_xhb, t_h2T, t_aT = [T(n) for n in
        ('xt', 'oct', 'oT', 'r1', 'xh', 'xhb', 'h2T', 'aT')]
    t_rep = t_aT
    t_rl = [T('rl0'), T('rl1')]
    t_small = T('small')
    t_psT, t_psbig = T('psT'), T('psbig')
    t_psa = [T('psa0'), T('psa1')]

    P.dma('sp', ident[:], identd, [], [t_ident], 'd_const')
    P.dma('sp', identf[:], identfd, [], [t_identf], 'd_const')
    P.dma('sp', scT[:], cT, [], [t_scT], 'd_const')
    for i in range(4):
        P.dma('sp', lnb[:, i, :], lnp[i:i + 1, :].to_broadcast([128, D]), [], [t_lnb], 'd_lnb')
    secdst = [(gb[:, 0:D], t_gb), (r1[:], t_r1), (xh[:], t_xh), (gb[:, D:2 * D], t_gb)]
    for sct in range(4):
        P.dma('sp', secdst[sct][0], bada[:, sct * D:(sct + 1) * D].to_broadcast([128, D]), [], [secdst[sct][1]],
              'd_modb%d' % sct)

    P.act(scT[:], scT[:], AF.Silu, reads=[t_scT], writes=[t_scT])
    P.cp('dve', rep, scT[:].unsqueeze(2).to_broadcast([128, 8, 128]), [t_scT], [t_rep])
    wada_v = wada.rearrange("(k p) n -> p k n", p=128)
    si = 0
    for jc in range(8):
        c0 = jc * 512
        pj, tpj = ps_a[jc % 2], t_psa[jc % 2]
        for kk in range(4):
            s = si % NST
            si += 1
            st = stage[s][:].rearrange("p (k n) -> p k n", k=2)
            P.dma('sp' if si % 2 == 0 else 'pool', st, wada_v[:, 2 * kk:2 * kk + 2, c0:c0 + 512], [], [t_stage[s]], 'd_st%d' % s)
            for k2 in range(2):
                k = 2 * kk + k2
                P.mm(pj[:, 0:512], rep[:, k, :], st[:, k2, :], k == 0, k == 7, [t_rep, t_stage[s]], [tpj])
        dst, tdst = secdst[c0 // D]
        cc = c0 % D
        P.tt('dve', dst[:, cc:cc + 512], pj[:, 0:512], dst[:, cc:cc + 512], ALU.add, [tpj, tdst], [tdst])
    P.ts('dve', gb[:], gb[:], 1.0, None, ALU.add, reads=[t_gb], writes=[t_gb])
    P.ts('dve', xh[:], xh[:], 1.0, None, ALU.add, reads=[t_xh], writes=[t_xh])
    P.tt('dve', xt[:], lnb[:, 0, :], xh[:], ALU.mult, [t_lnb, t_xh], [t_xt])
    P.tt('dve', xh[:], lnb[:, 1, :], xh[:], ALU.mult, [t_lnb, t_xh], [t_xh])
    P.tt('dve', xh[:], xh[:], r1[:], ALU.add, [t_xh, t_r1], [t_xh])
    idb = identf[:].unsqueeze(1).to_broadcast([128, 8, 128])
    r1v = r1[:].rearrange("p (k j) -> p k j", k=8)
    for i, (src, tsrc) in enumerate(((xt, t_xt), (xh, t_xh))):
        P.tt('dve', r1v, src[:].rearrange("p (k j) -> p k j", k=8), idb,
             ALU.mult, [tsrc, t_identf], [t_r1])
        P.op('dve', lambda e, i=i: e.reduce_sum(ABt[:, i, :], r1v, AX.X), [t_r1], [t_ABt])

    cast_engs = ['dve', 'pool', 'act']
    ci = 0

    def load_cast(dst3, src2, ncols, ttok):
        nonlocal si, ci
        K = dst3.shape[1]
        for k in range(K):
            for c0 in range(0, ncols, 1024):
                cw = min(1024, ncols - c0)
                s = si % NST
                si += 1
                P.dma('sp' if si % 2 == 0 else 'pool', stage[s][:, 0:cw], src2[k * 128:(k + 1) * 128, c0:c0 + cw],
                      [], [t_stage[s]], 'd_st%d' % s)
                P.cp(cast_engs[ci % 3], dst3[:, k, c0:c0 + cw], stage[s][:, 0:cw], [t_stage[s]], [ttok])
                ci += 1

    load_cast(wo_bf, wo, D, t_wo)
    load_cast(w1_bf, w1, DFF, t_w1)
    load_cast(w2_bf, w2, D, t_w2)

    xh2 = stage[0]
    st1b = stage[1][:].bitcast(BF16)
    xhs = [xh, xh2]
    t_xhs = [t_xh, t_stage[0]]
    xhbs = [xhb[:], st1b[:, 0:1024]]
    t_xhbs = [t_xhb, t_stage[1]]
    octs = [oct_[:], st1b[:, 1024:2048]]
    t_octs = [t_oct, T('oct2')]
    ps_T2 = P.ps("ps_T2", [128, 8, 128], BF16)
    ps_big2 = P.ps("ps_big2", [128, D], F32)
    t_psT2, t_psbig2 = T('psT2'), T('psbig2')

    def stageA(t):
        s = t % 2
        rows = slice(t * 128, (t + 1) * 128)
        xhc, t_xhc = xhs[s], t_xhs[s]
        P.dma('sp', xt[:], x[rows, :], [], [t_xt], 'd_xt')
        for k in range(8):
            P.tr(ps_T[:, k, :], octs[s][:, k * 128:(k + 1) * 128], ident[:], [t_octs[s], t_ident], [t_psT])
        P.cp('act', oT[:], ps_T[:], [t_psT], [t_oT])
        for n in range(2):
            for k in range(8):
                P.mm(ps_big[:, n * 512:(n + 1) * 512], oT[:, k, :], wo_bf[:, k, n * 512:(n + 1) * 512],
                     k == 0, k == 7, [t_oT, t_wo], [t_psbig])

        def chain():
            P.tt('dve', r1[:], ps_big[:], gb[:, 0:D], ALU.mult, [t_psbig, t_gb], [t_r1])
            P.stt(r1[:], xt[:], DN_ALPHA, r1[:], ALU.mult, ALU.add, reads=[t_xt, t_r1], writes=[t_r1])
            ln_tile(P, r1[:], xhc[:], stats, mv, sc, 1e-5, t_r1, t_xhc, t_small)
        P.atomic(chain)
        P.cp('act', xhbs[s], xhc[:], [t_xhc], [t_xhbs[s]])
        P.tt('dve', xhc[:], xhc[:], lnb[:, 0, :], ALU.mult, [t_xhc, t_lnb], [t_xhc])
        P.tt('dve', xhc[:], xhc[:], lnb[:, 1, :], ALU.add, [t_xhc, t_lnb], [t_xhc])

    def stageB(t):
        s = t % 2
        rows = slice(t * 128, (t + 1) * 128)
        xhc, t_xhc = xhs[s], t_xhs[s]
        for k in range(8):
            P.tr(ps_T2[:, k, :], xhbs[s][:, k * 128:(k + 1) * 128], ident[:], [t_xhbs[s], t_ident], [t_psT2])
        for k in range(8):
            P.ts('dve', h2T[:, k, :], ps_T2[:, k, :], ABt[:, 0, k:k + 1], ABt[:, 1, k:k + 1], ALU.mult, ALU.add,
                 reads=[t_psT2, t_ABt], writes=[t_h2T])
        for fg in range(8):
            pa = fg % 2
            for fi in range(4):
                f = fg * 4 + fi
                for k in range(8):
                    P.mm(ps_a[pa][:, fi * 128:(fi + 1) * 128], w1_bf[:, k, f * 128:(f + 1) * 128], h2T[:, k, :],
                         k == 0, k == 7, [t_w1, t_h2T], [t_psa[pa]])
            P.act(rl[pa][:], ps_a[pa][:], AF.Relu, reads=[t_psa[pa]], writes=[t_rl[pa]])
            P.tt('pool', aT[:, fg * 4:(fg + 1) * 4, :].rearrange("p a b -> p (a b)"), rl[pa][:], rl[pa][:], ALU.mult,
                 [t_rl[pa]], [t_aT])
        for n in range(2):
            for f in range(32):
                P.mm(ps_big2[:, n * 512:(n + 1) * 512], aT[:, f, :], w2_bf[:, f, n * 512:(n + 1) * 512],
                     f == 0, f == 31, [t_aT, t_w2], [t_psbig2])

    def stageB2(t):
        s = t % 2
        rows = slice(t * 128, (t + 1) * 128)
        xhc, t_xhc = xhs[s], t_xhs[s]

        def chain():
            P.tt('dve', r1[:], ps_big2[:], gb[:, D:2 * D], ALU.mult, [t_psbig2, t_gb], [t_r1])
            P.stt(r1[:], xhc[:], DN_ALPHA, r1[:], ALU.mult, ALU.add, reads=[t_xhc, t_r1], writes=[t_r1])
            ln_tile(P, r1[:], xhc[:], stats, mv, sc, 1e-5, t_r1, t_xhc, t_small)
        P.atomic(chain)
        P.tt('dve', xhc[:], xhc[:], lnb[:, 2, :], ALU.mult, [t_xhc, t_lnb], [t_xhc])
        P.tt('dve', xhc[:], xhc[:], lnb[:, 3, :], ALU.add, [t_xhc, t_lnb], [t_xhc])
        P.dma('sp', xo[rows, :], xhc[:], [t_xhc], [], 'd_out')

    def load_oct(t):
        P.dma('pool', octs[t % 2], oc[t * 128:(t + 1) * 128, :], [], [t_octs[t % 2]], 'd_oct%d' % (t % 2))

    load_oct(0)
    stageA(0)
    if NT > 1:
        load_oct(1)
    for t in range(NT):
        if t + 2 < NT:
            load_oct(t + 2)

        def side():
            if t > 0:
                stageB2(t - 1)
            if t + 1 < NT:
                stageA(t + 1)
        P.replay(P.record(lambda: stageB(t)), P.record(side), speeds=[1.0, FSPEED])
    stageB2(NT - 1)


def f_inputs(l, b, tok_slice, x_full, ocat_b, inp, consts):
    return {
        "x": np.ascontiguousarray(x_full[b, tok_slice, :]),
        "ocat": np.ascontiguousarray(ocat_b[tok_slice, :]),
        "cT": np.ascontiguousarray(inp['c'][b].reshape(8, 128).T),
        "wada": np.ascontiguousarray(inp['w_ada'][l][:, 2048:6144]),
        "bada": np.ascontiguousarray(inp['b_ada'][l][None, 2048:6144]),
        "wo": inp['w_o'][l], "w1": inp['w_ff1'][l], "w2": inp['w_ff2'][l],
        "lnp": np.stack([inp['ln1_w'][l], inp['ln1_b'][l], inp['ln2_w'][l], inp['ln2_b'][l]]),
        "ident": consts['ident'], "identf": consts['identf'],
    }


NEGB = -30000.0
SKIP = set()
GRAN = 1
FSPEED = 1.0
NCOL = 1548
HEAD_DIM = 64


def p_consts(S):
    c = {}
    c['ident'] = np.eye(128, dtype=np.float32).astype(NPBF)
    c['identf'] = np.eye(128, dtype=np.float32)
    s = np.arange(64)
    tri = (s[:, None] <= s[None, :]).astype(np.float32)
    c['tri64'] = np.ascontiguousarray(np.broadcast_to(tri[:, None, :], (64, 2, 64))).astype(np.float32)
    rm = np.ones((64, 512), np.float32)
    rm[:, ::64] = 0
    c['resetm'] = rm
    s = np.arange(128)
    c['tri128'] = (s[:, None] <= s[None, :]).astype(np.float32)
    causal = np.where(s[:, None] <= s[None, :], 0.0, NEGB).astype(np.float32)
    upper = np.where(s[:, None] > s[None, :], 0.0, NEGB).astype(np.float32)
    c['causal4'] = np.ascontiguousarray(np.broadcast_to(causal[:, None, :], (128, 4, 128))).astype(NPBF)
    c['upper4'] = np.ascontiguousarray(np.broadcast_to(upper[:, None, :], (128, 4, 128))).astype(NPBF)
    cb = np.zeros((128, 17, 128), np.float32)
    m = np.arange(128)[:, None]
    i = np.arange(128)[None, :]
    for a in range(16):
        cb[:, a, :] = np.where(i >= 16 * (m - 8 * a) + 31, 0.0, NEGB)
    cb[:, 16, :] = np.where((m == 127) & (i < 15), NEGB, 0.0)
    c['cbv'] = cb.astype(NPBF)
    key = np.arange(S)
    c['eall'] = (key[None, :] // 64 == np.arange(128)[:, None]).astype(np.float32).astype(NPBF)
    c['ehalf'] = ((key[None, :] // 64) % 64 == np.arange(64)[:, None]).astype(np.float32).astype(NPBF)
    nC = S // 16 - 1
    nS = S // 64
    cs = np.arange(nC) * 16
    ss = np.arange(nS) * 64
    ov = ((cs[:, None] < ss[None, :] + 64) & (ss[None, :] < cs[:, None] + 32)).astype(np.float32)
    OV = np.zeros((512, 128), np.float32)
    OV[:nC, :nS] = ov
    c['ov'] = np.ascontiguousarray(OV.reshape(4, 128, 128).transpose(1, 0, 2)).astype(NPBF)
    ii = np.arange(128)[:, None]
    jp = np.arange(256)[None, :] - 127
    F = np.zeros((128, 256), np.float32)
    F = np.where(jp > 1, -(100.0 + jp), F)
    F = np.where(jp == 1, np.where(ii < 64, -101.0, 110.0), F)
    F = np.where(jp == 0, np.where(ii < 64, 110.0, 120.0), F)
    F = np.where(jp == -1, np.where(ii < 64, 120.0, 0.0), F)
    c['fbig'] = F.astype(np.float32)
    invf = (10000.0 ** (-np.arange(0, 64, 2, dtype=np.float32) / 64)).astype(np.float32)
    c['invf'] = np.ascontiguousarray(np.broadcast_to(invf[None, :], (128, 32))).astype(np.float32)
    return c


def gelu_tanh(P, dst, x, tmp, eng, tx, tdst, ttmp):
    P.tt(eng, tmp, x, x, ALU.mult, [tx], [ttmp])
    P.ts(eng, tmp, tmp, 0.044715, 1.0, ALU.mult, ALU.add, reads=[ttmp], writes=[ttmp])
    P.tt(eng, tmp, tmp, x, ALU.mult, [ttmp, tx], [ttmp])
    P.act(tmp, tmp, AF.Sigmoid, scale=1.5957691216057308, reads=[ttmp], writes=[ttmp])
    P.tt(eng, dst, tmp, x, ALU.mult, [ttmp, tx], [tdst])


def build_P(S, stages=(1, 2, 3)):
    D = D_MODEL
    NT = S // 128
    nc = bass.Bass("TRN2", target_bir_lowering=False)

    def din(name, shape, dt=F32):
        return nc.dram_tensor(name, list(shape), dt, kind="ExternalInput").ap()

    A = NS({})
    A.x = din("x", [S, D])
    A.cT = din("cT", [128, 8])
    A.wada = din("wada", [D, 2048])
    A.badaT = din("badaT", [128, 16])
    A.win = din("win", [D, NCOL])
    A.posT = din("posT", [128, NT], I32)
    A.lbT = din("lbT", [64, 2, 4])
    A.lsel = din("lsel", [64, 2, 4])
    A.hnw = din("hnw", [1, 64])
    A.pek = din("pekT", [64, 32])
    A.pev = din("pevT", [64, 32])
    A.w1k = din("w1k", [2048, 256])
    A.w1v = din("w1v", [2048, 256])
    A.w2k = din("w2k", [256, 64])
    A.w2v = din("w2v", [256, 64])
    A.gnw = din("gnw", [1, 256])
    A.gnb = din("gnb", [1, 256])
    A.wsT = din("wsT", [2, 128, 128])
    A.bsT = din("bsT", [128, 2])
    A.c_ident = din("ident", [128, 128], BF16)
    A.c_identf = din("identf", [128, 128])
    A.c_tri64 = din("tri64", [64, 2, 64])
    A.c_resetm = din("resetm", [64, 512])
    A.c_tri128 = din("tri128", [128, 128])
    A.c_causal4 = din("causal4", [128, 4, 128], BF16)
    A.c_upper4 = din("upper4", [128, 4, 128], BF16)
    A.c_cbv = din("cbv", [128, 17, 128], BF16)
    A.c_ehalf = din("ehalf", [64, S], BF16)
    A.c_ov = din("ov", [128, 4, 128], BF16)
    A.c_fbig = din("fbig", [128, 256])
    A.c_invf = din("invf", [128, 32])
    o = nc.dram_tensor("o", [S, 512], BF16, kind="ExternalOutput").ap()
    A.o_h, A.o_n, A.o_g = o[:, 0:128], o[:, 128:384], o[:, 384:512]
    A.qTd = nc.dram_tensor("qTd", [NT, 64, 512], BF16).ap()
    P = Prog(nc)
    emit_P(nc, P, S, A, stages)
    P.emit()
    return nc


def emit_P(nc, P, S, A, stages=(1, 2, 3)):
    D = D_MODEL
    NT = S // 128
    NB = S // 512
    nC = S // 16 - 1
    x = A.x
    cT = A.cT
    wada = A.wada
    badaT = A.badaT
    win = A.win
    posT = A.posT
    lbT = A.lbT
    lsel = A.lsel
    hnw = A.hnw
    pek = A.pek
    pev = A.pev
    w1k = A.w1k
    w1v = A.w1v
    w2k = A.w2k
    w2v = A.w2v
    gnw = A.gnw
    gnb = A.gnb
    wsT = A.wsT
    bsT = A.bsT
    c_ident = A.c_ident
    c_identf = A.c_identf
    c_tri64 = A.c_tri64
    c_resetm = A.c_resetm
    c_tri128 = A.c_tri128
    c_causal4 = A.c_causal4
    c_upper4 = A.c_upper4
    c_cbv = A.c_cbv
    c_ehalf = A.c_ehalf
    c_ov = A.c_ov
    c_fbig = A.c_fbig
    c_invf = A.c_invf
    o_h, o_n, o_g, qTd = A.o_h, A.o_n, A.o_g, A.qTd
    T = P.tok
    ident = P.sb("ident_sb", [128, 128], BF16)
    identf = P.sb("identf_sb", [128, 128], F32)
    win_bf = P.sb("win_bf", [128, 8, NCOL], BF16)
    kT = P.sb("kT", [128, 4, S], BF16)
    vsl = P.sb("vsl", [128, NT, 65], BF16)
    vwn = P.sb("vwn", [128, NT, 65], BF16)
    gates = P.sb("gates", [128, NT, 12], F32)
    cs = P.sb("cs", [128, NT, 64], F32)
    modT = P.sb("modT", [128, 16], F32)
    t_ident, t_identf, t_win, t_kT, t_vsl, t_vwn, t_gates, t_cs, t_modT = [T(n) for n in
        ('ident', 'identf', 'win', 'kT', 'vsl', 'vwn', 'gates', 'cs', 'modT')]
    P.dma('sp', ident[:], c_ident, [], [t_ident], 'd_c0')
    P.dma('sp', identf[:], c_identf, [], [t_identf], 'd_c0')
    P.dma('sp', kT[64:128, 1, :], c_ehalf, [], [t_kT], 'd_c0')
    P.op('pool', lambda e: e.memset(kT[64:128, 2, :], 0.0), [], [t_kT])

    es1 = ExitStack()
    es0 = ExitStack()
    sb0 = lambda n, sh, dt: es0.enter_context(nc.sbuf_tensor(P.pfx + "a0_" + n, list(sh), dt))
    sb1 = lambda n, sh, dt: es1.enter_context(nc.sbuf_tensor(P.pfx + "a1_" + n, list(sh), dt))
    ps1 = lambda n, sh, dt: es1.enter_context(nc.psum_tensor(P.pfx + "q1_" + n, list(sh), dt))
    lb = P.sb("lb", [64, 2], F32)
    oml = P.sb("oml", [64, 2], F32)
    noml = P.sb("noml", [64, 2], F32)
    tri64 = P.sb("tri64", [64, 2, 64], F32)
    resetm = P.sb("resetm", [64, 512], F32)
    tri128 = P.sb("tri128", [128, 128], F32)
    hnwb = P.sb("hnwb", [64, 2, 64], F32)
    gnwb = P.sb("gnwb", [128, 256], F32)
    gnbb = P.sb("gnbb", [128, 256], F32)
    ws_bf = P.sb("ws_bf", [128, 2, 128], BF16)
    bs = P.sb("bs", [128, 2], F32)
    stage = [sb0("stage%d" % i, [128, NCOL], F32) for i in range(2)]
    t_stage = [T('st0'), T('st1')]
    scT = sb0("scT", [128, 8], F32)
    t_scT = T('scT')
    pA = [ps1("pA%d" % i, [128, 512], F32) for i in range(3)]
    t_pA = [T('pA%d' % i) for i in range(3)]
    pT = ps1("pT", [128, 1024], BF16)
    t_pT = T('pT')
    pTf = ps1("pTf", [128, 512], F32)
    t_pTf = T('pTf')
    pH_au = ps1("pH_au", [128, 512], F32)
    pH_attn = pH_au[0:64, 0:128].rearrange("p (h t) -> p h t", h=2)
    pH_U = pH_au[0:64, 128:256].rearrange("p (h t) -> p h t", h=2)
    pH_o = ps1("pH_o", [128, 512], F32)[0:64, 0:128]
    t_pHa, t_pHo = T('pHa'), T('pHo')
    t_pHU = t_pHa
    pT2 = ps1("pT2", [128, 1024], BF16)
    t_pT2 = T('pT2')
    pai = {'h': 0, 't': 0}
    pools = {'h': [0], 't': [1, 2]}

    def nextpA(stream='t'):
        pl = pools[stream]
        i = pl[pai[stream] % len(pl)]
        pai[stream] += 1
        return pA[i], t_pA[i]

    P.dma('sp', scT[:], cT, [], [t_scT], 'd_c1')
    P.dma('sp', modT[:], badaT, [], [t_modT], 'd_c1')
    P.act(scT[:], scT[:], AF.Silu, reads=[t_scT], writes=[t_scT])
    wada_v = wada.rearrange("(k p) n -> p k n", p=128)
    for j in range(16):
        s = j % 2
        st = stage[s][:, 0:1024].rearrange("p (k n) -> p k n", k=8)
        P.dma('sp', st, wada_v[:, :, j * 128:(j + 1) * 128], [], [t_stage[s]], 'd_st%d' % s)
        ps_, tps_ = nextpA()
        for k in range(8):
            P.mm(ps_[:, 0:1], st[:, k, :], scT[:, k:k + 1], k == 0, k == 7, [t_stage[s], t_scT], [tps_])
        P.tt('dve', modT[:, j:j + 1], ps_[:, 0:1], modT[:, j:j + 1], ALU.add, [tps_, t_modT], [t_modT])
    P.ts('dve', modT[:, 8:16], modT[:, 8:16], 1.0, None, ALU.add, reads=[t_modT], writes=[t_modT])

    for k in range(8):
        s = k % 2
        P.dma('sp' if s == 0 else 'pool', stage[s][:], win[k * 128:(k + 1) * 128, :], [], [t_stage[s]], 'd_st%d' % s)
        P.cp(['dve', 'pool'][k % 2], win_bf[:, k, :], stage[s][:], [t_stage[s]], [t_win])

    posi = sb0("posi", [128, NT], I32)
    posf = sb0("posf", [128, NT], F32)
    invf = sb0("invf", [128, 32], F32)
    ang = sb0("ang", [128, NT, 32], F32)
    rr = sb0("rr", [128, NT, 32], F32)
    ki = sb0("ki", [128, NT, 32], I32)
    kf = sb0("kf", [128, NT, 32], F32)
    t_rope = T('rope')
    P.dma('sp', posi[:], posT, [], [t_rope], 'd_c2')
    P.dma('sp', invf[:], c_invf, [], [t_rope], 'd_c2')
    P.cp('dve', posf[:], posi[:], [t_rope], [t_rope])
    P.tt('dve', ang[:], posf[:].unsqueeze(2).to_broadcast([128, NT, 32]),
         invf[:].unsqueeze(1).to_broadcast([128, NT, 32]), ALU.mult, [t_rope], [t_rope])
    TWO_PI = 2.0 * np.pi
    C1 = 6.28125
    C2 = TWO_PI - C1
    for which in (1, 0):
        src = ang
        if which == 0:
            P.ts('dve', rr[:], ang[:], float(np.pi / 2), None, ALU.add, reads=[t_rope], writes=[t_rope])
            src = rr
        P.ts('dve', kf[:], src[:], float(1.0 / TWO_PI), None, ALU.mult, reads=[t_rope], writes=[t_rope])
        P.cp('dve', ki[:], kf[:], [t_rope], [t_rope])
        P.cp('dve', kf[:], ki[:], [t_rope], [t_rope])
        P.stt(rr[:], kf[:], -C1, src[:], ALU.mult, ALU.add, reads=[t_rope], writes=[t_rope])
        P.stt(rr[:], kf[:], -C2, rr[:], ALU.mult, ALU.add, reads=[t_rope], writes=[t_rope])
        P.ts('dve', kf[:], rr[:], float(np.pi), float(-TWO_PI), ALU.is_gt, ALU.mult, reads=[t_rope], writes=[t_rope])
        P.tt('dve', rr[:], rr[:], kf[:], ALU.add, [t_rope], [t_rope])
        P.ts('dve', kf[:], rr[:], float(-np.pi), float(TWO_PI), ALU.is_lt, ALU.mult, reads=[t_rope], writes=[t_rope])
        P.tt('dve', rr[:], rr[:], kf[:], ALU.add, [t_rope], [t_rope])
        P.act(cs[:, :, which * 32:(which + 1) * 32], rr[:], AF.Sin, reads=[t_rope], writes=[t_cs])

    lbr = sb0("lbr", [64, 2, 4], F32)
    lsl = sb0("lsl", [64, 2, 4], F32)
    lbs = sb0("lbs", [64, 2], F32)
    t_lb = T('lb')
    P.dma('sp', lbr[:], lbT, [], [t_lb], 'd_c3')
    P.dma('sp', lsl[:], lsel, [], [t_lb], 'd_c3')
    P.act(lbr[:], lbr[:], AF.Exp, reads=[t_lb], writes=[t_lb])
    P.op('dve', lambda e: e.reduce_sum(lbs[:], lbr[:], AX.X), [t_lb], [t_lb])
    P.op('dve', lambda e: e.reciprocal(lbs[:], lbs[:]), [t_lb], [t_lb])
    P.tt('dve', lbr[:], lbr[:], lsl[:], ALU.mult, [t_lb], [t_lb])
    P.op('dve', lambda e: e.reduce_sum(lb[:], lbr[:], AX.X), [t_lb], [t_lb])
    P.tt('dve', lb[:], lb[:], lbs[:], ALU.mult, [t_lb], [t_lb])
    P.ts('dve', oml[:], lb[:], -1.0, 1.0, ALU.mult, ALU.add, reads=[t_lb], writes=[t_lb])
    P.ts('dve', noml[:], oml[:], -1.0, None, ALU.mult, reads=[t_lb], writes=[t_lb])

    wsf = sb0("wsf", [128, 2, 128], F32)
    t_c1 = T('c1')
    P.dma('sp', tri64[:], c_tri64, [], [t_c1], 'd_c4')
    P.dma('sp', resetm[:], c_resetm, [], [t_c1], 'd_c4')
    P.dma('sp', tri128[:], c_tri128, [], [t_c1], 'd_c4')
    for h in range(2):
        P.dma('sp', hnwb[:, h, :], hnw.to_broadcast([64, 64]), [], [t_c1], 'd_c4')
        P.dma('sp', wsf[:, h, :], wsT[h], [], [t_c1], 'd_c4')
    P.dma('sp', gnwb[:], gnw.to_broadcast([128, 256]), [], [t_c1], 'd_c4')
    P.dma('sp', gnbb[:], gnb.to_broadcast([128, 256]), [], [t_c1], 'd_c4')
    P.dma('sp', bs[:], bsT, [], [t_c1], 'd_c4')
    P.tt('dve', ws_bf[:], wsf[:], tri128[:].unsqueeze(1).to_broadcast([128, 2, 128]), ALU.mult, [t_c1], [t_c1])
    P.op('pool', lambda e: e.memset(vsl[:, :, 64:65], 1.0), [], [t_vsl])
    P.op('pool', lambda e: e.memset(vwn[:, :, 64:65], 1.0), [], [t_vwn])

    P.barrier()
    es0.close()
    xt = [sb1("xt%d" % i, [128, D], F32) for i in range(2)]
    t_xt = [T('xt0'), T('xt1')]
    hTs = [sb1("hT%d" % i, [128, 8, 512], BF16) for i in range(2)]
    t_hTs = [T('hT0'), T('hT1')]
    hq, hsg, hlf, hb, hkk, he = [sb1(n, [64, 2, 512], F32) for n in ('hq', 'hsg', 'hlf', 'hb', 'hkk', 'he')]
    t_hq, t_hsg, t_hlf, t_hb, t_hkk, t_he = [T(n) for n in ('hq', 'hsg', 'hlf', 'hb', 'hkk', 'he')]
    AT, BT, CqT, CkT, iTb, sgT = [sb1(n, [64, 2, 512], BF16) for n in ('AT', 'BT', 'CqT', 'CkT', 'iTb', 'sgT')]
    t_AT, t_BT, t_CqT, t_CkT, t_iTb, t_sgT = [T(n) for n in ('AT', 'BT', 'CqT', 'CkT', 'iTb', 'sgT')]
    dcol = sb1("dcol", [64, 2, 8], F32)
    t_dcol = T('dcol')
    Sst = sb1("Sst", [64, 2, 64], F32)
    S_bf = sb1("S_bf", [64, 2, 64], BF16)
    t_S, t_Sbf = T('S'), T('Sbf')
    tok3 = sb1("tok3", [64, 6, 64], BF16)
    t_tok3 = T('tok3')
    attn_bf = sb1("attn_bf", [64, 2, 64], BF16)
    t_attn = T('attn')
    osb = sb1("osb", [64, 128], F32)
    osq = sb1("osq", [64, 128], F32)
    oss = sb1("oss", [64, 4], F32)
    ohb = sb1("ohb", [64, 128], BF16)
    t_osb, t_osq, t_oss, t_ohb = T('osb'), T('osq'), T('oss'), T('ohb')
    rt1 = sb1("rt1", [128, 7, 32], F32)
    rt2 = sb1("rt2", [128, 7, 32], F32)
    qk_tm = sb1("qk_tm", [128, 8, 64], BF16)
    t_rt1, t_rt2, t_qk = T('rt1'), T('rt2'), T('qk')
    qTt = [sb1("qTt%d" % i, [64, 512], BF16) for i in range(2)]
    t_qTt = [T('qTt0'), T('qTt1')]
    gx = sb1("gx", [128, 384], F32)
    gtmp = sb1("gtmp", [128, 384], F32)
    gg = sb1("gg", [128, 384], F32)
    gvh = sb1("gvh", [128, 256], F32)
    gvn = sb1("gvn", [128, 128], BF16)
    ogb = sb1("ogb", [128, 128], BF16)
    gstats = sb1("gstats", [128, 6], F32)
    gmv = sb1("gmv", [128, 2], F32)
    gsc = sb1("gsc", [128, 4], F32)
    t_gx, t_gtmp, t_gg, t_gvh, t_gvn, t_ogb, t_gsm = [T(n) for n in ('gx', 'gtmp', 'gg', 'gvh', 'gvn', 'ogb', 'gsm')]

    P.op('dve', lambda e: e.memset(Sst[:], 0.0), [], [t_S])
    P.op('dve', lambda e: e.memset(S_bf[:], 0.0), [], [t_Sbf])

    def part_hT(blk):
        hT, t_hT = hTs[blk % 2], t_hTs[blk % 2]
        for ti in range(4):
            t = blk * 4 + ti
            xb = xt[t % 2]
            txb = t_xt[t % 2]
            P.dma('sp' if t % 2 == 0 else 'pool', xb[:], x[t * 128:(t + 1) * 128, :], [], [txb], 'd_xt%d' % (t % 2))
            for kh in range(2):
                for k4 in range(4):
                    k = kh * 4 + k4
                    P.tr(pTf[:, k4 * 128:(k4 + 1) * 128], xb[:, k * 128:(k + 1) * 128], identf[:], [txb, t_identf], [t_pTf])
                for k4 in range(4):
                    k = kh * 4 + k4
                    P.act(hT[:, k, ti * 128:(ti + 1) * 128], pTf[:, k4 * 128:(k4 + 1) * 128], AF.Identity,
                          scale=modT[:, 8 + k:9 + k], bias=modT[:, k:k + 1], reads=[t_pTf, t_modT], writes=[t_hT])

    def part_hgrn(blk):
        hT, t_hT = hTs[blk % 2], t_hTs[blk % 2]
        for h in (range(2) if 'hgrn' not in SKIP else ()):
            def proj(qi):
                ps_, tps_ = nextpA('h')
                c0 = qi * 128 + h * 64
                for k in range(8):
                    P.mm(ps_[0:64, :], win_bf[:, k, c0:c0 + 64], hT[:, k, :], k == 0, k == 7, [t_win, t_hT], [tps_])
                return ps_, tps_
            ps_, tps_ = proj(0)
            P.act(hq[:, h, :], ps_[0:64, :], AF.Silu, reads=[tps_], writes=[t_hq])
            ps_, tps_ = proj(3)
            P.act(sgT[:, h, :], ps_[0:64, :], AF.Silu, reads=[tps_], writes=[t_sgT])
            ps_, tps_ = proj(1)
            P.act(hsg[:, h, :], ps_[0:64, :], AF.Sigmoid, reads=[tps_], writes=[t_hsg])
            ps_, tps_ = proj(2)
            P.cp('dve', iTb[:, h, :], ps_[0:64, :], [tps_], [t_iTb])
            P.ts('dve', hlf[:, h, :], hsg[:, h, :], oml[:, h:h + 1], lb[:, h:h + 1], ALU.mult, ALU.add,
                 reads=[t_hsg, t_lb], writes=[t_hlf])
            P.ts('dve', hlf[:, h, :], hlf[:, h, :], 1e-30, None, ALU.max, reads=[t_hlf], writes=[t_hlf])
            P.ts('dve', hkk[:, h, :], hsg[:, h, :], noml[:, h:h + 1], oml[:, h:h + 1], ALU.mult, ALU.add,
                 reads=[t_hsg, t_lb], writes=[t_hkk])
        if 'hgrn' not in SKIP:
            P.act(hlf[:], hlf[:], AF.Ln, reads=[t_hlf], writes=[t_hlf])
            for h in range(2):
                P.op('dve', lambda e, h=h: e.tensor_tensor_scan(hb[:, h, :], resetm[:], hlf[:, h, :], 0.0, ALU.mult, ALU.add),
                     [t_hlf, t_c1], [t_hb])
            hbv = hb[:].rearrange("p h (c t) -> p (h c) t", t=64)
            hev = he[:].rearrange("p h (c t) -> p (h c) t", t=64)
            rmid = hbv[:, :, 31:32].to_broadcast([64, 16, 64])
            rlast = hbv[:, :, 63:64].to_broadcast([64, 16, 64])
            he2 = he[:].rearrange("p h t -> p (h t)")
            hb2 = hb[:].rearrange("p h t -> p (h t)")
            hq2 = hq[:].rearrange("p h t -> p (h t)")
            hkk2 = hkk[:].rearrange("p h t -> p (h t)")
            f2 = lambda a: a[:].rearrange("p h t -> p (h t)")
            P.tt('dve', hev, hbv, rmid, ALU.subtract, [t_hb], [t_he])
            P.ts('dve', he2, he2, 43.0, None, ALU.min, reads=[t_he], writes=[t_he])
            P.act(he2, he2, AF.Exp, reads=[t_he], writes=[t_he])
            P.tt('dve', f2(AT), he2, hq2, ALU.mult, [t_he, t_hq], [t_AT])
            P.tt('dve', hev, rmid, hbv, ALU.subtract, [t_hb], [t_he])
            P.ts('dve', he2, he2, 43.0, None, ALU.min, reads=[t_he], writes=[t_he])
            P.act(he2, he2, AF.Exp, reads=[t_he], writes=[t_he])
            P.tt('dve', f2(BT), he2, hkk2, ALU.mult, [t_he, t_hkk], [t_BT])
            P.act(he2, hb2, AF.Exp, reads=[t_hb], writes=[t_he])
            P.tt('dve', f2(CqT), he2, hq2, ALU.mult, [t_he, t_hq], [t_CqT])
            P.tt('dve', hev, rlast, hbv, ALU.subtract, [t_hb], [t_he])
            P.act(he2, he2, AF.Exp, reads=[t_he], writes=[t_he])
            P.tt('dve', f2(CkT), he2, hkk2, ALU.mult, [t_he, t_hkk], [t_CkT])
            P.act(dcol[:].rearrange("p h c -> p (h c)"), hbv[:, :, 63], AF.Exp, reads=[t_hb], writes=[t_dcol])

        for c in (range(8) if ('hgrn' not in SKIP and 'hchunks' not in SKIP) else ()):
            csl = slice(c * 64, (c + 1) * 64)
            row0 = blk * 512 + c * 64
            for h in range(2):
                P.tr(pT[0:64, (0 + h) * 64:(1 + h) * 64], iTb[:, h, csl], ident[0:64, 0:64], [t_iTb, t_ident], [t_pT])
                P.tr(pT[0:64, (2 + h) * 64:(3 + h) * 64], CkT[:, h, csl], ident[0:64, 0:64], [t_CkT, t_ident], [t_pT])
                P.tr(pT[0:64, (4 + h) * 64:(5 + h) * 64], sgT[:, h, csl], ident[0:64, 0:64], [t_sgT, t_ident], [t_pT])
            P.cp('act', tok3[:].rearrange("p a b -> p (a b)"), pT[0:64, 0:384], [t_pT], [t_tok3])
            for h in range(2):
                P.mm(pH_attn[:, h, :], BT[:, h, csl], AT[:, h, csl], True, True, [t_BT, t_AT], [t_pHa])
            P.tt('dve', attn_bf[:], pH_attn[:], tri64[:], ALU.mult, [t_pHa, t_c1], [t_attn])
            for h in range(2):
                P.mm(pH_o[:, h * 64:(h + 1) * 64], attn_bf[:, h, :], tok3[:, h, :], True, False, [t_attn, t_tok3], [t_pHo])
                P.mm(pH_o[:, h * 64:(h + 1) * 64], CqT[:, h, csl], S_bf[:, h, :], False, True, [t_CqT, t_Sbf], [t_pHo])
            for h in range(2):
                P.mm(pH_U[:, h, :], tok3[:, 2 + h, :], tok3[:, h, :], True, True, [t_tok3], [t_pHU])
            for h in range(2):
                P.stt(Sst[:, h, :], Sst[:, h, :], dcol[:, h, c:c + 1], pH_U[:, h, :], ALU.mult, ALU.add,
                      reads=[t_S, t_dcol, t_pHU], writes=[t_S])
            P.cp('pool', S_bf[:], Sst[:], [t_S], [t_Sbf])
            P.cp('act', osb[:], pH_o[:], [t_pHo], [t_osb])
            P.tt('dve', osq[:], osb[:], osb[:], ALU.mult, [t_osb], [t_osq])
            P.op('dve', lambda e: e.reduce_sum(oss[:, 0:2], osq[:].rearrange("p (h v) -> p h v", h=2), AX.X), [t_osq], [t_oss])
            P.ts('dve', oss[:, 0:2], oss[:, 0:2], 1.0 / 64, 1e-6, ALU.mult, ALU.add, reads=[t_oss], writes=[t_oss])
            P.act(oss[:, 0:2], oss[:, 0:2], AF.Sqrt, reads=[t_oss], writes=[t_oss])
            P.op('dve', lambda e: e.reciprocal(oss[:, 2:4], oss[:, 0:2]), [t_oss], [t_oss])
            o3 = osb[:].rearrange("p (h v) -> p h v", h=2)
            P.tt('dve', o3, o3, oss[:, 2:4].unsqueeze(2).to_broadcast([64, 2, 64]), ALU.mult, [t_osb, t_oss], [t_osb])
            P.tt('dve', o3, o3, hnwb[:], ALU.mult, [t_osb, t_c1], [t_osb])
            P.tt('dve', ohb[:].rearrange("p (h v) -> p h v", h=2), o3, tok3[:, 4:6, :], ALU.mult, [t_osb, t_tok3], [t_ohb])
            P.dma('sp', o_h[row0:row0 + 64, :], ohb[:], [t_ohb], [], 'd_oh')


    def part_tok(blk):
        hT, t_hT = hTs[blk % 2], t_hTs[blk % 2]
        for ti in (range(4) if 'tok' not in SKIP else ()):
            t = blk * 4 + ti
            tsl = slice(ti * 128, (ti + 1) * 128)
            rows = slice(t * 128, (t + 1) * 128)
            if 'nsa_tm' not in SKIP:
                psB, tpsB = nextpA()
                for k in range(8):
                    P.mm(psB[:], hT[:, k, tsl], win_bf[:, k, 512:1024], k == 0, k == 7, [t_hT, t_win], [tpsB])
                B4 = psB[:, 0:448].rearrange("p (a two d) -> p a two d", two=2, d=32)
                a1 = B4[:, :, 0, :]
                a2 = B4[:, :, 1, :]
                cosb = cs[:, t, 0:32].unsqueeze(1).to_broadcast([128, 7, 32])
                sinb = cs[:, t, 32:64].unsqueeze(1).to_broadcast([128, 7, 32])
                Q4 = qk_tm[:, 0:7, :].rearrange("p a (two d) -> p a two d", two=2)
                P.tt('dve', rt1[:], a1, cosb, ALU.mult, [tpsB, t_cs], [t_rt1])
                P.tt('dve', rt2[:], a2, sinb, ALU.mult, [tpsB, t_cs], [t_rt2])
                P.tt('dve', Q4[:, :, 0, :], rt1[:], rt2[:], ALU.subtract, [t_rt1, t_rt2], [t_qk])
                P.tt('dve', rt1[:], a2, cosb, ALU.mult, [tpsB, t_cs], [t_rt1])
                P.tt('dve', rt2[:], a1, sinb, ALU.mult, [tpsB, t_cs], [t_rt2])
                P.tt('dve', Q4[:, :, 1, :], rt1[:], rt2[:], ALU.add, [t_rt1, t_rt2], [t_qk])
                P.cp('act', qk_tm[:, 7, :], psB[:, 448:512], [tpsB], [t_qk])
                if 'cut1' not in SKIP:
                    for a in range(8):
                        P.tr(pT2[0:64, a * 128:(a + 1) * 128], qk_tm[:, a, :], ident[:], [t_qk, t_ident], [t_pT2])
                    qs = t % 2
                    P.cp('act', qTt[qs][:], pT2[0:64, 0:512], [t_pT2], [t_qTt[qs]])
                    P.cp('pool' if False else 'dve', kT[0:64, :, rows], pT2[0:64, 512:1024].rearrange("p (a t) -> p a t", a=4), [t_pT2], [t_kT])
                if 'cut2' not in SKIP and 'cut1' not in SKIP:
                    P.dma('sp', qTd[t], qTt[qs][:], [t_qTt[qs]], [], 'd_qT%d' % qs)
                if 'cut3' not in SKIP:
                    psC, tpsC = nextpA()
                    for k in range(8):
                        P.mm(psC[:, 0:140], hT[:, k, tsl], win_bf[:, k, 1024:1164], k == 0, k == 7, [t_hT, t_win], [tpsC])
                    P.cp('act', vsl[:, t, 0:64], psC[:, 0:64], [tpsC], [t_vsl])
                    P.cp('act', vwn[:, t, 0:64], psC[:, 64:128], [tpsC], [t_vwn])
                    P.act(gates[:, t, :], psC[:, 128:140], AF.Sigmoid, reads=[tpsC], writes=[t_gates])
            if 'gmlp' not in SKIP:
                psG, tpsG = nextpA()
                for k in range(8):
                    P.mm(psG[:, 0:384], hT[:, k, tsl], win_bf[:, k, 1164:1548], k == 0, k == 7, [t_hT, t_win], [tpsG])
                P.cp('act', gx[:], psG[:, 0:384], [tpsG], [t_gx])
                gelu_tanh(P, gg[:], gx[:], gtmp[:], 'pool', t_gx, t_gg, t_gtmp)
                P.op('dve', lambda e: e.bn_stats(gstats[:], gg[:, 128:384]), [t_gg], [t_gsm])
                P.op('dve', lambda e: e.bn_aggr(gmv[:], gstats[:]), [t_gsm], [t_gsm])
                P.ts('dve', gsc[:, 0:1], gmv[:, 1:2], 1e-5, None, ALU.add, reads=[t_gsm], writes=[t_gsm])
                P.act(gsc[:, 0:1], gsc[:, 0:1], AF.Sqrt, reads=[t_gsm], writes=[t_gsm])
                P.op('dve', lambda e: e.reciprocal(gsc[:, 1:2], gsc[:, 0:1]), [t_gsm], [t_gsm])
                P.stt(gsc[:, 2:3], gmv[:, 0:1], -1.0, gsc[:, 1:2], ALU.mult, ALU.mult, reads=[t_gsm], writes=[t_gsm])
                P.act(gvh[:], gg[:, 128:384], AF.Identity, scale=gsc[:, 1:2], bias=gsc[:, 2:3], reads=[t_gg, t_gsm], writes=[t_gvh])
                P.tt('dve', gvh[:, 0:128], gvh[:, 0:128], gnwb[:, 0:128], ALU.mult, [t_gvh, t_c1], [t_gvh])
                P.tt('dve', gvn[:], gvh[:, 0:128], gnbb[:, 0:128], ALU.add, [t_gvh, t_c1], [t_gvn])
                psS, tpsS = nextpA()
                for g in range(2):
                    P.mm(psS[:, g * 64:(g + 1) * 64], ws_bf[:, g, :], gvn[:, g * 64:(g + 1) * 64], True, True, [t_c1, t_gvn], [tpsS])
                for g in range(2):
                    P.stt(ogb[:, g * 64:(g + 1) * 64], psS[:, g * 64:(g + 1) * 64], bs[:, g:g + 1], gg[:, g * 64:(g + 1) * 64],
                          ALU.add, ALU.mult, reads=[tpsS, t_c1, t_gg], writes=[t_ogb])
                P.dma('sp', o_g[rows, :], ogb[:], [t_ogb], [], 'd_og')


    if 1 in stages:
        part_hT(0)
        for blk in range(NB):
            streams = [P.record(lambda: part_hgrn(blk)), P.record(lambda: part_tok(blk))]
            if blk + 1 < NB:
                streams.append(P.record(lambda: part_hT(blk + 1)))
            P.replay(*streams, gran=GRAN)
    P.barrier()
    es1.close()
    build_P23(nc, P, S, stages, locals())


def build_P23(nc, P, S, stages, env):
    v = NS(env)
    T = P.tok
    NT = S // 128
    nC = S // 16 - 1
    ident, identf, kT, vsl, vwn, gates = v.ident, v.identf, v.kT, v.vsl, v.vwn, v.gates
    t_ident, t_identf, t_kT, t_vsl, t_vwn, t_gates = v.t_ident, v.t_identf, v.t_kT, v.t_vsl, v.t_vwn, v.t_gates
    qTd = v.qTd
    sb = P.sb
    ps = P.ps
    kcT = sb("kcT", [128, 512], BF16)
    R = sb("R", [128, 4, 65], BF16)
    OV = sb("OV", [128, 4, 128], BF16)
    t_kcT, t_R, t_OV = T('kcT'), T('R'), T('OV')
    P.dma('sp', OV[:], v.c_ov, [], [t_OV], 'd_c5')
    P.op('pool', lambda e: e.memset(kcT[:], 0.0), [], [t_kcT])
    P.op('pool', lambda e: e.memset(R[:, :, 0:64], 0.0), [], [t_R])
    P.op('pool', lambda e: e.memset(R[:, :, 64:65], 1.0), [], [t_R])
    es2 = ExitStack()
    sb2 = lambda n, sh, dt: es2.enter_context(nc.sbuf_tensor(P.pfx + "a2_" + n, list(sh), dt))
    ps2 = lambda n, sh, dt: es2.enter_context(nc.psum_tensor(P.pfx + "q2_" + n, list(sh), dt))
    w1st = sb2("w1st", [64, 8, 256], F32)
    w1b = [sb2("w1b%d" % i, [64, 32, 256], BF16) for i in range(2)]
    w2st = sb2("w2st", [128, 2, 64], F32)
    w2b = [sb2("w2b%d" % i, [128, 2, 64], BF16) for i in range(2)]
    peT = sb2("peT", [64, 2, 32], F32)
    peTb = sb2("peTb", [64, 2, 32], BF16)
    pbias = sb2("pbias", [128, 2, 2], F32)
    hx = sb2("hx", [128, 512], F32)
    htmp = sb2("htmp", [128, 512], F32)
    hTb = sb2("hTb", [128, 2, 512], BF16)
    t_w1st, t_w2st, t_pe, t_pbias, t_hx, t_htmp, t_hTb = [T(n) for n in ('w1st', 'w2st', 'pe', 'pbias', 'hx', 'htmp', 'hTb')]
    t_w1b = [T('w1b0'), T('w1b1')]
    t_w2b = [T('w2b0'), T('w2b1')]
    pc = [ps2("pc%d" % i, [128, 512], F32) for i in range(2)]
    t_pc = [T('pc0'), T('pc1')]
    if 2 in stages:
        P.dma('sp', peT[:, 0, :], v.pek, [], [t_pe], 'd_c6')
        P.dma('sp', peT[:, 1, :], v.pev, [], [t_pe], 'd_c6')
        P.cp('dve', peTb[:], peT[:], [t_pe], [t_pe])
        for kv, (w1d, w2d) in enumerate(((v.w1k, v.w2k), (v.w1v, v.w2v))):
            w1v_ = w1d.rearrange("(j d) h -> d j h", d=64)
            for jq in range(4):
                P.dma('sp', w1st[:], w1v_[:, jq * 8:(jq + 1) * 8, :], [], [t_w1st], 'd_w1st')
                P.cp(['dve', 'pool'][jq % 2], w1b[kv][:, jq * 8:(jq + 1) * 8, :], w1st[:], [t_w1st], [t_w1b[kv]])
            P.dma('sp', w2st[:], w2d.rearrange("(c p) d -> p c d", p=128), [], [t_w2st], 'd_w2st')
            P.cp('dve', w2b[kv][:], w2st[:], [t_w2st], [t_w2b[kv]])
            for c2 in range(2):
                for j in range(32):
                    P.mm(pc[0][:, 0:1], w1b[kv][:, j, c2 * 128:(c2 + 1) * 128], peTb[:, kv, j:j + 1], j == 0, j == 31,
                         [t_w1b[kv], t_pe], [t_pc[0]])
                P.cp('dve', pbias[:, kv, c2:c2 + 1], pc[0][:, 0:1], [t_pc[0]], [t_pbias])
            src = kT[0:64, 0, :] if kv == 0 else kT[0:64, 3, :]
            P.op('pool', lambda e: e.memset(hTb[:], 0.0), [], [t_hTb])
            for c2 in range(2):
                pp, tpp = pc[c2 % 2], t_pc[c2 % 2]
                for j in range(32):
                    P.mm(pp[:, 0:nC], w1b[kv][:, j, c2 * 128:(c2 + 1) * 128], src[:, j:j + 16 * (nC - 1) + 1:16], j == 0, j == 31,
                         [t_w1b[kv], t_kT], [tpp])
                P.act(hx[:, 0:nC], pp[:, 0:nC], AF.Identity, bias=pbias[:, kv, c2:c2 + 1], reads=[tpp, t_pbias], writes=[t_hx])
                gelu_tanh(P, hTb[:, c2, 0:nC], hx[:, 0:nC], htmp[:, 0:nC], 'dve', t_hx, t_hTb, t_htmp)
            if kv == 0:
                for c2 in range(2):
                    P.mm(pc[0][0:64, 0:nC], w2b[kv][:, c2, :], hTb[:, c2, 0:nC], c2 == 0, c2 == 1, [t_w2b[kv], t_hTb], [t_pc[0]])
                P.cp('act', kcT[0:64, 0:nC], pc[0][0:64, 0:nC], [t_pc[0]], [t_kcT])
            else:
                for ntile in range((nC + 127) // 128):
                    for c2 in range(2):
                        P.mm(pc[1][:, 0:64], hTb[:, c2, ntile * 128:(ntile + 1) * 128], w2b[kv][:, c2, :], c2 == 0, c2 == 1,
                             [t_hTb, t_w2b[kv]], [t_pc[1]])
                    P.cp('act', R[:, ntile, 0:64], pc[1][:, 0:64], [t_pc[1]], [t_R])
    P.barrier()
    es2.close()
    if 3 not in stages:
        return
    causal4 = sb("causal4", [128, 4, 128], BF16)
    upper4 = sb("upper4", [128, 4, 128], BF16)
    cbv = sb("cbv", [128, 17, 128], BF16)
    fbig = sb("fbig", [128, 256], F32)
    t_c3 = T('c3')
    P.dma('sp', causal4[:], v.c_causal4, [], [t_c3], 'd_c7')
    P.dma('sp', upper4[:], v.c_upper4, [], [t_c3], 'd_c7')
    P.dma('sp', cbv[:], v.c_cbv, [], [t_c3], 'd_c7')
    P.dma('sp', fbig[:], v.c_fbig, [], [t_c3], 'd_c7')
    qt = [sb("qt%d" % i, [128, 512], BF16) for i in range(2)]
    qtB = [sb("qtB%d" % i, [128, 512], BF16) for i in range(2)]
    t_qt = [T('qt0'), T('qt1')]
    t_qtB = [T('qtB0'), T('qtB1')]
    negb3 = sb("negb3", [128, 192], BF16)
    P.op('pool', lambda e: e.memset(negb3[:], 0.0), [], [T('negb3i')])
    for i in range(2):
        P.op('pool', lambda e, i=i: e.memset(qt[i][:], 0.0), [], [t_qt[i]])
        P.op('pool', lambda e, i=i: e.memset(qtB[i][:], 0.0), [], [t_qtB[i]])
    cb4 = sb("cb4", [128, 4, 128], BF16)
    cb4p = sb("cb4p", [128, 4, 128], BF16)
    t_cb4, t_cb4p = T('cb4'), T('cb4p')
    P.cp('pool', cb4p[:], cbv[:, 16, :].unsqueeze(1).to_broadcast([128, 4, 128]), [t_c3], [t_cb4p])
    osbT = sb("osbT", [65, 512], F32)
    osbT2 = sb("osbT2", [65, 512], F32)
    t_osbT2 = T('osbT2')
    ty1 = sb("ty1", [128, 128], F32)
    ty2 = sb("ty2", [128, 128], F32)
    t_ty1, t_ty2 = T('ty1'), T('ty2')
    impsb = sb("impsb", [128, 512], F32)
    t_osbT, t_impsb = T('osbT'), T('impsb')
    zz = sb("zz", [128, 3, 4], F32)
    t_zz = [T('zz0'), T('zz1'), T('zz2')]
    accs = [sb("acc%d" % i, [128, 4, 64], F32) for i in range(2)]
    t_accs = [T('acc0'), T('acc1')]
    accb = sb("accb", [128, 256], BF16)
    t_accb = T('accb')
    imp = sb("imp", [128, 128], F32)
    sc2 = sb("sc2", [128, 128], F32)
    m8 = sb("m8", [128, 16], F32)
    negb = sb("negb", [128, 128], BF16)
    nsT4 = sb("nsT4", [128, 4, 128], BF16)
    t_imp, t_sc2, t_m8, t_negb, t_nsT4 = [T(n) for n in ('imp', 'sc2', 'm8', 'negb', 'nsT4')]
    s_ps = [ps("s_ps%d" % i, [128, 512], F32) for i in range(3)]
    t_sps = [T('p_sps%d' % i) for i in range(3)]
    oT_ps = [ps("oT_ps%d" % i, [128, 512], F32) for i in range(3)]
    t_oT = [T('p_oT0'), T('p_oT1'), T('p_oT2')]
    impT_ps = ps("impT_ps", [128, 512], F32)
    t_impT = T('p_impT')
    tpA = ps("tpA", [128, 512], F32)
    t_tpA = T('p_tpA')
    tpB = impT_ps[:].bitcast(BF16)
    t_tpB = t_impT
    tpA2 = tpA
    t_tpA2 = t_tpA
    si = [0]

    class Stream:
        def __init__(self, name, npb):
            self.pend = []
            self.pb = [sb("pb%s%d" % (name, i), [128, 512], BF16) for i in range(npb)]
            self.t_pb = [T('pb%s%d' % (name, i)) for i in range(npb)]
            self.pi = 0

        def flush(self):
            while self.pend:
                self.pend.pop(0)()

    SX = Stream('X', 3)
    SY = Stream('Y', 4)

    def score_tile(st, lhsT, lhs_toks, masks, qtile, tq):
        i = si[0] % 3
        si[0] += 1
        sp_, tsp = s_ps[i], t_sps[i]
        j = st.pi % len(st.pb)
        st.pi += 1

        def grp():
            P.mm(sp_[:], lhsT, qtile, True, len(masks) == 0, lhs_toks + [tq], [tsp])
            for mi, (ml, mr, mt) in enumerate(masks):
                P.mm(sp_[:], ml, mr, False, mi == len(masks) - 1, mt, [tsp])
            P.act(st.pb[j][:], sp_[:], AF.Exp, scale=0.125, reads=[tsp], writes=[st.t_pb[j]])
        P.atomic(grp)
        while len(st.pend) > 1:
            st.pend.pop(0)()
        return st.pb[j], st.t_pb[j]

    def finish_branch(br, tp_, t_tp, osb_, t_osb):
        def grp():
            P.cp('act', osb_[:], oT_ps[br][0:65, :], [t_oT[br]], [t_osb])
            for g in range(4):
                P.tr(tp_[:, g * 65:(g + 1) * 65], osb_[:, g * 128:(g + 1) * 128], identf[0:65, 0:65], [t_osb, t_identf], [t_tp])
        P.atomic(grp)
        tv = tp_[:, 0:260].rearrange("p (g c) -> p g c", g=4)
        P.ts('dve', zz[:, br, :], tv[:, :, 64], 1e-30, None, ALU.max, reads=[t_tp], writes=[t_zz[br]])
        P.op('dve', lambda e: e.reciprocal(zz[:, br, :], zz[:, br, :]), [t_zz[br]], [t_zz[br]])
        return tv

    def partX(qb):
        qs = qb % 2
        P.dma('sp', qt[qs][0:64, :], qTd[qb], [], [t_qt[qs]], 'd_qt%d' % qs)
        if qb >= 32:
            P.dma('sp', qtB[qs][0:64, :], qTd[qb], [], [t_qtB[qs]], 'd_qtB%d' % qs)
        a = qb % 16
        ntl = qb // 16
        P.cp('pool', cb4[:], cbv[:, a, :].unsqueeze(1).to_broadcast([128, 4, 128]), [t_c3], [t_cb4])
        for nt in range(ntl + 1):
            masks = []
            if nt == ntl:
                masks.append((ident[:], cb4[:].rearrange("p g q -> p (g q)"), [t_ident, t_cb4]))
            elif nt == ntl - 1 and a == 0:
                masks.append((ident[:], cb4p[:].rearrange("p g q -> p (g q)"), [t_ident, t_cb4p]))
            pt, tpt = score_tile(SX, kcT[:, nt * 128:(nt + 1) * 128], [t_kcT], masks, qt[qs][:], t_qt[qs])
            def pv(nt=nt, pt=pt, tpt=tpt):
                P.mm(oT_ps[0][0:65, :], R[:, nt, :], pt[:], nt == 0, nt == ntl, [t_R, tpt], [t_oT[0]])
                P.mm(impT_ps[:], OV[:, nt, :], pt[:], nt == 0, nt == ntl, [t_OV, tpt], [t_impT])
            SX.pend.append(pv)
        SX.flush()
        def xfin():
            tv = finish_branch(0, tpA, t_tpA, osbT, t_osbT)
            gv_ = gates[:, qb, :].rearrange("p (g c) -> p g c", c=3)
            P.tt('dve', zz[:, 0, :], zz[:, 0, :], gv_[:, :, 0], ALU.mult, [t_zz[0], t_gates], [t_zz[0]])
            P.cp('act', impsb[:], impT_ps[:], [t_impT], [t_impsb])
            P.tt('dve', accs[qs][:], tv[:, :, 0:64], zz[:, 0, :].unsqueeze(2).to_broadcast([128, 4, 64]), ALU.mult,
                 [t_tpA, t_zz[0]], [t_accs[qs]])
            P.ts('dve', m8[:, 8:12], tv[:, :, 64], 1e-30, None, ALU.max, reads=[t_tpA], writes=[t_m8])
            P.op('dve', lambda e: e.reciprocal(m8[:, 8:12], m8[:, 8:12]), [t_m8], [t_m8])
            for g in range(4):
                P.tr(tpA[:, g * 128:(g + 1) * 128], impsb[:, g * 128:(g + 1) * 128], identf[:], [t_impsb, t_identf, t_accs[qs], t_m8], [t_tpA])
            P.ts('dve', imp[:], tpA[:, 0:128], m8[:, 8:9], None, ALU.mult, reads=[t_tpA, t_m8], writes=[t_imp])
            for g in range(1, 4):
                P.stt(imp[:], tpA[:, g * 128:(g + 1) * 128], m8[:, 8 + g:9 + g], imp[:], ALU.mult, ALU.add,
                      reads=[t_tpA, t_m8, t_imp], writes=[t_imp])
        P.atomic(xfin)
        P.tt('dve', imp[:], imp[:], fbig[:, 127 - 2 * qb:255 - 2 * qb], ALU.add, [t_imp, t_c3], [t_imp])
        P.ts('dve', imp[:, 0:1], imp[:, 0:1], 100.0, None, ALU.add, reads=[t_imp], writes=[t_imp])
        P.op('dve', lambda e: e.max(m8[:, 0:8], imp[:]), [t_imp], [t_m8])
        P.op('dve', lambda e: e.match_replace(sc2[:], m8[:, 0:8], imp[:], -1e9), [t_imp, t_m8], [t_sc2])
        P.op('dve', lambda e: e.max(m8[:, 0:8], sc2[:]), [t_sc2], [t_m8])
        P.ts('dve', sc2[:], imp[:], m8[:, 7:8], None, ALU.is_ge, reads=[t_imp, t_m8], writes=[t_sc2])
        P.ts('dve', negb3[:, 64:192], sc2[:], -1.0, -NEGB, ALU.add, ALU.mult, reads=[t_sc2], writes=[t_negb])
        P.tr(tpB[:, 0:128], negb3[:, 0:128], ident[:], [t_negb, t_ident], [t_tpB])
        if qb >= 32:
            P.tr(tpB[:, 128:256], negb3[:, 64:192], ident[:], [t_negb, t_ident], [t_tpB])
        P.cp('dve', qt[qs][64:128, :].rearrange("p (g q) -> p g q", g=4),
             tpB[64:128, 0:128].unsqueeze(1).to_broadcast([64, 4, 128]), [t_tpB], [t_qt[qs]])
        if qb >= 32:
            P.cp('dve', qtB[qs][64:128, :].rearrange("p (g q) -> p g q", g=4),
                 tpB[64:128, 128:256].unsqueeze(1).to_broadcast([64, 4, 128]), [t_tpB], [t_qtB[qs]])

    def partY(qb):
        qs = qb % 2
        for kt in range(qb + 1):
            masks = []
            if kt == qb:
                masks.append((ident[:], causal4[:].rearrange("p g q -> p (g q)"), [t_ident, t_c3]))
            qq, tqq = (qt[qs], t_qt[qs]) if kt < 32 else (qtB[qs], t_qtB[qs])
            pt, tpt = score_tile(SY, kT[:, 1, kt * 128:(kt + 1) * 128], [t_kT], masks, qq[:], tqq)
            SY.pend.append(lambda kt=kt, pt=pt, tpt=tpt: P.mm(oT_ps[1][0:65, :], vsl[:, kt, :], pt[:], kt == 0, kt == qb,
                                                           [t_vsl, tpt], [t_oT[1]]))
        k0 = max(0, qb - 4)
        for kt in range(k0, qb + 1):
            masks = []
            if kt == qb:
                masks.append((ident[:], causal4[:].rearrange("p g q -> p (g q)"), [t_ident, t_c3]))
            elif kt == qb - 4:
                masks.append((ident[:], upper4[:].rearrange("p g q -> p (g q)"), [t_ident, t_c3]))
            pt, tpt = score_tile(SY, kT[:, 2, kt * 128:(kt + 1) * 128], [t_kT], masks, qt[qs][:], t_qt[qs])
            SY.pend.append(lambda kt=kt, pt=pt, tpt=tpt: P.mm(oT_ps[2][0:65, :], vwn[:, kt, :], pt[:], kt == k0, kt == qb,
                                                           [t_vwn, tpt], [t_oT[2]]))
        SY.flush()
        for br in (1, 2):
            def yfin(br=br):
                tv = finish_branch(br, tpA2, t_tpA2, osbT2, t_osbT2)
                P.tt('dve', zz[:, br, :], zz[:, br, :], gates[:, qb, :].rearrange("p (g c) -> p g c", c=3)[:, :, br], ALU.mult, [t_zz[br], t_gates], [t_zz[br]])
                P.tt('dve', ty1[:].rearrange("p (g c) -> p g c", g=4)[:, :, 0:32], tv[:, :, 0:32],
                     zz[:, br, :].unsqueeze(2).to_broadcast([128, 4, 32]), ALU.mult, [t_tpA2, t_zz[br]], [t_ty1])
                P.tt('dve', ty2[:].rearrange("p (g c) -> p g c", g=4), tv[:, :, 32:64],
                     zz[:, br, :].unsqueeze(2).to_broadcast([128, 4, 32]), ALU.mult, [t_tpA2, t_zz[br]], [t_ty2])
                P.tt('dve', accs[qs][:, :, 0:32], accs[qs][:, :, 0:32], ty1[:].rearrange("p (g c) -> p g c", g=4), ALU.add, [t_accs[qs], t_ty1], [t_accs[qs]])
                P.tt('dve', accs[qs][:, :, 32:64], accs[qs][:, :, 32:64], ty2[:].rearrange("p (g c) -> p g c", g=4), ALU.add, [t_accs[qs], t_ty2], [t_accs[qs]])
            P.atomic(yfin)
        P.cp('act', accb[:], accs[qs][:].rearrange("p g c -> p (g c)"), [t_accs[qs]], [t_accb])
        P.dma('sp', v.o_n[qb * 128:(qb + 1) * 128, :], accb[:], [t_accb], [], 'd_on')

    partX(0)
    for qb in range(NT):
        streams = [P.record(lambda: partY(qb))]
        if qb + 1 < NT:
            streams.append(P.record(lambda: partX(qb + 1)))
        P.replay(*streams)


def p_inputs(l, b, hh, x_full, inp, consts):
    S = x_full.shape[1]
    NT = S // 128
    win = inp['w_in'][l]
    h0 = hh * 128
    cols = []
    for q in range(4):
        cols.append(np.arange(q * 256 + h0, q * 256 + h0 + 128))
    cols.append(np.arange(1024 + hh * 256, 1024 + hh * 256 + 256))
    for base in (1536, 1792, 2048, 1664):
        cols.append(np.arange(base + hh * 64, base + hh * 64 + 64))
    for base in (1920, 2176):
        cols.append(np.arange(base + hh * 64, base + hh * 64 + 64))
    cols.append(np.arange(2304 + hh * 12, 2304 + hh * 12 + 12))
    cols.append(np.arange(2328 + hh * 128, 2328 + hh * 128 + 128))
    gperm = np.concatenate([np.arange(hh * 128, hh * 128 + 128), np.arange((1 - hh) * 128, (1 - hh) * 128 + 128)])
    cols.append(2584 + gperm)
    cols = np.concatenate(cols)
    assert cols.shape[0] == NCOL
    lb = inp['hgrn_lower_bounds']
    lbT = np.ascontiguousarray(lb[:, h0:h0 + 128].reshape(4, 2, 64).transpose(2, 1, 0))
    lsel = np.zeros((64, 2, 4), np.float32)
    lsel[:, :, 1:l + 1] = 1.0
    d = {
        "x": np.ascontiguousarray(x_full[b]),
        "cT": np.ascontiguousarray(inp['c'][b].reshape(8, 128).T),
        "wada": np.ascontiguousarray(inp['w_ada'][l][:, 0:2048]),
        "badaT": np.ascontiguousarray(inp['b_ada'][l][0:2048].reshape(16, 128).T),
        "win": np.ascontiguousarray(win[:, cols]),
        "posT": np.ascontiguousarray(inp['positions'][b].reshape(NT, 128).T.astype(np.int32)),
        "lbT": lbT.astype(np.float32), "lsel": lsel,
        "hnw": np.ascontiguousarray(inp['hgrn_norm_w'][l][None, :]),
        "pekT": np.ascontiguousarray(inp['cmp_pe_k'][l].T), "pevT": np.ascontiguousarray(inp['cmp_pe_v'][l].T),
        "w1k": inp['cmp_w1_k'][l], "w1v": inp['cmp_w1_v'][l], "w2k": inp['cmp_w2_k'][l], "w2v": inp['cmp_w2_v'][l],
        "gnw": np.ascontiguousarray(inp['gmlp_norm_w'][l][gperm][None, :]),
        "gnb": np.ascontiguousarray(inp['gmlp_norm_b'][l][gperm][None, :]),
        "wsT": np.ascontiguousarray(inp['gmlp_w_s'][l][2 * hh:2 * hh + 2].transpose(0, 2, 1)),
        "bsT": np.ascontiguousarray(inp['gmlp_b_s'][l][2 * hh:2 * hh + 2].T),
    }
    for k in C_KEYS:
        d[k] = consts[k]
    return d


def assemble_ocat(o0, o1):
    return np.concatenate([o0[:, 0:128], o1[:, 0:128], o0[:, 128:384], o1[:, 128:384], o0[:, 384:512], o1[:, 384:512]], axis=1)


_CACHE = {}

P_KEYS_LH = ('win', 'gnw', 'gnb', 'wsT', 'bsT')
P_KEYS_L = ('wada', 'badaT', 'lsel', 'hnw', 'pekT', 'pevT', 'w1k', 'w1v', 'w2k', 'w2v')
P_KEYS_H = ('lbT',)
F_KEYS_L = ('wada', 'bada', 'wo', 'w1', 'w2', 'lnp')
C_KEYS = ('ident', 'identf', 'tri64', 'resetm', 'tri128', 'causal4', 'upper4', 'cbv', 'ehalf', 'ov', 'fbig', 'invf')
P_NAME = {'pekT': 'pek', 'pevT': 'pev'}


def build_fused(S):
    D = D_MODEL
    NT = S // 128
    L = DEPTH
    nc = bass.Bass("TRN2", target_bir_lowering=False)

    def din(name, shape, dt=F32):
        return nc.dram_tensor(name, list(shape), dt, kind="ExternalInput").ap()

    shp = {'win': [D, NCOL], 'gnw': [1, 256], 'gnb': [1, 256], 'wsT': [2, 128, 128], 'bsT': [128, 2],
           'wada': [D, 2048], 'badaT': [128, 16], 'lsel': [64, 2, 4], 'hnw': [1, 64], 'pekT': [64, 32], 'pevT': [64, 32],
           'w1k': [2048, 256], 'w1v': [2048, 256], 'w2k': [256, 64], 'w2v': [256, 64], 'lbT': [64, 2, 4]}
    fshp = {'wada': [D, 4096], 'bada': [1, 4096], 'wo': [D, D], 'w1': [D, D_FF], 'w2': [D_FF, D], 'lnp': [4, D]}
    cshp = {'ident': ([128, 128], BF16), 'identf': ([128, 128], F32), 'tri64': ([64, 2, 64], F32), 'resetm': ([64, 512], F32),
            'tri128': ([128, 128], F32), 'causal4': ([128, 4, 128], BF16), 'upper4': ([128, 4, 128], BF16),
            'cbv': ([128, 17, 128], BF16), 'ehalf': ([64, S], BF16), 'ov': ([128, 4, 128], BF16), 'fbig': ([128, 256], F32),
            'invf': ([128, 32], F32)}
    x_in = din("x", [S, D])
    cT = din("cT", [128, 8])
    posT = din("posT", [128, NT], I32)
    dp = {}
    for k in P_KEYS_LH:
        dp[k] = din("p_" + k, [L, 2] + shp[k])
    for k in P_KEYS_L:
        dp[k] = din("p_" + k, [L] + shp[k])
    for k in P_KEYS_H:
        dp[k] = din("p_" + k, [2] + shp[k])
    df = {k: din("f_" + k, [L] + fshp[k]) for k in F_KEYS_L}
    dc = {k: din(k, cshp[k][0], cshp[k][1]) for k in C_KEYS}
    xbuf = [nc.dram_tensor("xbuf%d" % i, [S, D], F32).ap() for i in range(2)]
    ocat_d = nc.dram_tensor("ocat_d", [S, D], BF16).ap()
    qTd = nc.dram_tensor("qTd", [NT, 64, 512], BF16).ap()
    xo = nc.dram_tensor("xo", [S, D], F32, kind="ExternalOutput").ap()

    P = Prog(nc)
    for l in range(L):
        xsrc = x_in if l == 0 else xbuf[(l - 1) % 2]
        xdst = xo if l == L - 1 else xbuf[l % 2]
        for hh in range(2):
            A = NS({})
            A.x, A.cT, A.posT = xsrc, cT, posT
            for k in P_KEYS_LH:
                setattr(A, P_NAME.get(k, k), dp[k][l, hh])
            for k in P_KEYS_L:
                setattr(A, P_NAME.get(k, k), dp[k][l])
            for k in P_KEYS_H:
                setattr(A, P_NAME.get(k, k), dp[k][hh])
            for k in C_KEYS:
                setattr(A, 'c_' + k, dc[k])
            A.o_h = ocat_d[:, hh * 128:(hh + 1) * 128]
            A.o_n = ocat_d[:, 256 + hh * 256:256 + (hh + 1) * 256]
            A.o_g = ocat_d[:, 768 + hh * 128:768 + (hh + 1) * 128]
            A.qTd = qTd
            P.begin_phase("L%dh%d_" % (l, hh))
            emit_P(nc, P, S, A)
            P.end_phase()
        A = NS({})
        A.x, A.oc, A.cT, A.xo = xsrc, ocat_d, cT, xdst
        A.wada, A.bada, A.wo, A.w1, A.w2, A.lnp = [df[k][l] for k in F_KEYS_L]
        A.identd, A.identfd = dc['ident'], dc['identf']
        P.begin_phase("L%df_" % l)
        emit_F(nc, P, S, A)
        P.end_phase()
    P.emit()
    return nc


def fused_inputs(b, inp, consts):
    S = inp['x'].shape[1]
    per = [[p_inputs(l, b, hh, inp['x'], inp, consts) for hh in range(2)] for l in range(DEPTH)]
    d = {"x": per[0][0]['x'], "cT": per[0][0]['cT'], "posT": per[0][0]['posT']}
    for k in P_KEYS_LH:
        d["p_" + k] = np.stack([np.stack([per[l][hh][k] for hh in range(2)]) for l in range(DEPTH)])
    for k in P_KEYS_L:
        d["p_" + k] = np.stack([per[l][0][k] for l in range(DEPTH)])
    for k in P_KEYS_H:
        d["p_" + k] = np.stack([per[0][hh][k] for hh in range(2)])
    for l in range(DEPTH):
        pass
    fl = [f_inputs(l, b, slice(0, 1), inp['x'], np.zeros((1, D_MODEL), NPBF), inp, consts) for l in range(DEPTH)]
    for k in F_KEYS_L:
        d["f_" + k] = np.stack([fl[l][k] for l in range(DEPTH)])
    for k in C_KEYS:
        d[k] = consts[k]
    return d


def kernel(**inputs):
    inp = {k: np.asarray(v) for k, v in inputs.items()}
    inp['x'] = np.ascontiguousarray(inp['x'], dtype=np.float32)
    B, S, D = inp['x'].shape
    n = 8
    if 'fused' not in _CACHE:
        _CACHE['fused'] = build_fused(S)
        _CACHE['c'] = p_consts(S)
    nc, consts = _CACHE['fused'], _CACHE['c']
    maps = [fused_inputs(b, inp, consts) for b in range(B)]
    in_maps = [maps[c % B] for c in range(n)]
    res = run_bass_kernel_spmd(nc, in_maps, core_ids=list(range(n)))
    return np.stack([res.results[b]['xo'] for b in range(B)]).astype(np.float32)
```
